# Optimizing a Trainium2 kernel written in Bass

```python
import math
import jax, jax.numpy as jnp
from jax import lax
import numpy as np

D_MODEL = 1024
BATCH = 8
SEQ = 2048
DEPTH = 2
DEC_BATCH = 4
DEC_SEQ = 8192
PAST_LEN = 128

N_MIXERS = 2
GRID_W = 64
N_HEADS = 16
HEAD_DIM = D_MODEL // N_HEADS
WIN_ROWS = 8
WIN_COLS = 16
D_RNN = D_MODEL
N_BLOCKS = 16
BLOCK = D_RNN // N_BLOCKS
CONV_W = 4
CONV_LEFT = 2
LRU_C = 8.0
D_FF = 4 * D_MODEL
EPS = 1e-6

kernel_name = "hybrid_rglru_natten_encoder"


def rms_norm(x, g):
    xf = x.astype(jnp.float32)
    y = xf * lax.rsqrt(jnp.mean(xf * xf, axis=-1, keepdims=True) + EPS)
    return (y * g.astype(jnp.float32)).astype(x.dtype)


def centred_dwconv(x, w, b):
    S = x.shape[1]
    xp = jnp.pad(x, ((0, 0), (CONV_LEFT, CONV_W - 1 - CONV_LEFT), (0, 0)))
    y = b
    for k in range(CONV_W):
        y = y + xp[:, k:k + S] * w[k]
    return y


def block_diag_linear(x, w, b):
    B, S, _ = x.shape
    xb = x.reshape(B, S, N_BLOCKS, BLOCK)
    y = jnp.einsum('bsni,nij->bsnj', xb, w) + b
    return y.reshape(B, S, D_RNN)


def _lin_combine(left, right):
    a1, b1 = left
    a2, b2 = right
    return (a1 * a2, a2 * b1 + b2)


def rglru_direction(x, wa, ba, wx, bx, lam, reverse):
    r = jax.nn.sigmoid(block_diag_linear(x, wa, ba).astype(jnp.float32))
    i = jax.nn.sigmoid(block_diag_linear(x, wx, bx).astype(jnp.float32))
    log_a = LRU_C * r * jax.nn.log_sigmoid(lam.astype(jnp.float32))
    a = jnp.exp(log_a)
    mult = jnp.sqrt(-jnp.expm1(2.0 * log_a))
    b = mult * i * x.astype(jnp.float32)
    _, h = lax.associative_scan(_lin_combine, (a, b), axis=1, reverse=reverse)
    return h


def rglru_mixer(x, w_in, conv_w, conv_b,
                fwd_wa, fwd_ba, fwd_wx, fwd_bx, fwd_lam,
                bwd_wa, bwd_ba, bwd_wx, bwd_bx, bwd_lam, w_out):
    u = x @ w_in
    xr, gate = u[..., :D_RNN], u[..., D_RNN:]
    xr = centred_dwconv(xr, conv_w, conv_b)
    h = (rglru_direction(xr, fwd_wa, fwd_ba, fwd_wx, fwd_bx, fwd_lam, False)
         + rglru_direction(xr, bwd_wa, bwd_ba, bwd_wx, bwd_bx, bwd_lam, True))
    return (h.astype(x.dtype) * jax.nn.gelu(gate)) @ w_out


def na_mixer(x, w_qkv, rpb, w_o):
    B, S, D = x.shape
    rows = S // GRID_W
    kh = min(WIN_ROWS, rows)
    qkv = (x @ w_qkv).reshape(B, rows, GRID_W, 3, N_HEADS, HEAD_DIM)
    q = qkv[:, :, :, 0] * (HEAD_DIM ** -0.5)
    k = qkv[:, :, :, 1]
    v = qkv[:, :, :, 2]
    cols = np.arange(GRID_W)
    cs = np.clip(cols - WIN_COLS // 2, 0, GRID_W - WIN_COLS)
    col_idx = cs[:, None] + np.arange(WIN_COLS)[None, :]
    col_off = col_idx - cols[:, None]
    rpb_col = rpb[:, :, col_off + WIN_COLS - 1]

    def row_step(args):
        r, q_r = args
        rs = jnp.clip(r - kh // 2, 0, rows - kh)
        k_nb = lax.dynamic_slice_in_dim(k, rs, kh, axis=1)[:, :, col_idx]
        v_nb = lax.dynamic_slice_in_dim(v, rs, kh, axis=1)[:, :, col_idx]
        s = jnp.einsum('bchd,bicjhd->bhcij', q_r, k_nb).astype(jnp.float32)
        row_off = rs + jnp.arange(kh) - r
        bias = rpb_col[:, row_off + WIN_ROWS - 1]
        s = s + jnp.transpose(bias, (0, 2, 1, 3)).astype(jnp.float32)[None]
        p = jax.nn.softmax(s.reshape(B, N_HEADS, GRID_W, kh * WIN_COLS), axis=-1)
        p = p.reshape(B, N_HEADS, GRID_W, kh, WIN_COLS).astype(v.dtype)
        return jnp.einsum('bhcij,bicjhd->bchd', p, v_nb)

    o = lax.map(row_step, (jnp.arange(rows), jnp.moveaxis(q, 1, 0)))
    o = jnp.moveaxis(o, 0, 1).reshape(B, S, D)
    return o @ w_o


def sq_relu_mlp(x, w_up, w_down):
    h = jax.nn.relu(x @ w_up)
    return (h * h) @ w_down


def setup_inputs(seed: int = 0) -> dict:
    key = jax.random.key(seed)
    ks = iter(jax.random.split(key, 40))

    def nrm(shape, scale):
        return jax.random.normal(next(ks), shape, jnp.float32) * scale

    def gain(n):
        return 1.0 + 0.02 * jax.random.normal(next(ks), (n,), jnp.float32)

    def lam():
        a0 = jax.random.uniform(next(ks), (D_RNN,), jnp.float32, minval=0.9, maxval=0.999)
        return jnp.log(a0) - jnp.log1p(-a0)

    d = {}
    d['x_prompt'] = nrm((BATCH, SEQ, D_MODEL), 1.0)
    d['x_sample'] = nrm((DEC_BATCH, DEC_SEQ, D_MODEL), 1.0)
    d['l0_norm_mix'] = gain(D_MODEL)
    d['l0_w_in'] = nrm((D_MODEL, 2 * D_RNN), D_MODEL ** -0.5)
    d['l0_conv_w'] = nrm((CONV_W, D_RNN), CONV_W ** -0.5)
    d['l0_conv_b'] = nrm((D_RNN,), 0.01)
    for dirn in ('fwd', 'bwd'):
        d['l0_%s_wa' % dirn] = nrm((N_BLOCKS, BLOCK, BLOCK), BLOCK ** -0.5)
        d['l0_%s_ba' % dirn] = nrm((N_BLOCKS, BLOCK), 0.01)
        d['l0_%s_wx' % dirn] = nrm((N_BLOCKS, BLOCK, BLOCK), BLOCK ** -0.5)
        d['l0_%s_bx' % dirn] = nrm((N_BLOCKS, BLOCK), 0.01)
        d['l0_%s_lam' % dirn] = lam()
    d['l0_w_out'] = nrm((D_RNN, D_MODEL), D_RNN ** -0.5)
    d['l0_norm_ffn'] = gain(D_MODEL)
    d['l0_w_up'] = nrm((D_MODEL, D_FF), D_MODEL ** -0.5)
    d['l0_w_down'] = nrm((D_FF, D_MODEL), D_FF ** -0.5)
    d['l1_norm_mix'] = gain(D_MODEL)
    d['l1_w_qkv'] = nrm((D_MODEL, 3 * D_MODEL), D_MODEL ** -0.5)
    d['l1_rpb'] = nrm((N_HEADS, 2 * WIN_ROWS - 1, 2 * WIN_COLS - 1), 0.1)
    d['l1_w_o'] = nrm((D_MODEL, D_MODEL), D_MODEL ** -0.5)
    d['l1_norm_ffn'] = gain(D_MODEL)
    d['l1_w_up'] = nrm((D_MODEL, D_FF), D_MODEL ** -0.5)
    d['l1_w_down'] = nrm((D_FF, D_MODEL), D_FF ** -0.5)
    d['final_norm'] = gain(D_MODEL)
    return d


def reference(x_prompt, x_sample,
              l0_norm_mix, l0_w_in, l0_conv_w, l0_conv_b,
              l0_fwd_wa, l0_fwd_ba, l0_fwd_wx, l0_fwd_bx, l0_fwd_lam,
              l0_bwd_wa, l0_bwd_ba, l0_bwd_wx, l0_bwd_bx, l0_bwd_lam,
              l0_w_out, l0_norm_ffn, l0_w_up, l0_w_down,
              l1_norm_mix, l1_w_qkv, l1_rpb, l1_w_o, l1_norm_ffn, l1_w_up, l1_w_down,
              final_norm):
    def mix0(h):
        return rglru_mixer(h, l0_w_in, l0_conv_w, l0_conv_b,
                           l0_fwd_wa, l0_fwd_ba, l0_fwd_wx, l0_fwd_bx, l0_fwd_lam,
                           l0_bwd_wa, l0_bwd_ba, l0_bwd_wx, l0_bwd_bx, l0_bwd_lam, l0_w_out)

    def mix1(h):
        return na_mixer(h, l1_w_qkv, l1_rpb, l1_w_o)

    layers = [
        (mix0, l0_norm_mix, l0_norm_ffn, l0_w_up, l0_w_down),
        (mix1, l1_norm_mix, l1_norm_ffn, l1_w_up, l1_w_down),
    ]

    def trunk(x):
        for i in range(DEPTH):
            mix, g_mix, g_ffn, w_up, w_down = layers[i]
            x = x + mix(rms_norm(x, g_mix))
            x = x + sq_relu_mlp(rms_norm(x, g_ffn), w_up, w_down)
        return rms_norm(x, final_norm)

    y_prompt = trunk(x_prompt)
    y_sample = trunk(x_sample)
    return (y_prompt, y_sample)
```

```python
import contextlib
import numpy as np
import concourse.bass as bass
import concourse.mybir as mybir
from concourse.bass_utils import run_bass_kernel_spmd

F32 = mybir.dt.float32
BF16 = mybir.dt.bfloat16
AF = mybir.ActivationFunctionType
ALU = mybir.AluOpType

D = 1024
T = 512
NEG = -30000.0
EPS = 1e-6
GC0 = 0.7978845608028654
GC1 = 0.044715

ENGS = ("pe", "act", "dve", "pool", "sp")
DMA_SLOTS = {"sp": 16, "pool": 10, "act": 2}


def _flat(lst):
    out = []
    for x in lst:
        if isinstance(x, (list, tuple)):
            out.extend(_flat(x))
        else:
            out.append(x)
    return out


class Buf:
    __slots__ = ("name", "lw", "rd")

    def __init__(self, name):
        self.name = name
        self.lw = None
        self.rd = []


class Op:
    __slots__ = ("eng", "emit", "deps", "is_dma", "milestone", "sem", "val", "idx")

    def __init__(self, eng, emit, is_dma):
        self.eng = eng
        self.emit = emit
        self.deps = set()
        self.is_dma = is_dma
        self.milestone = False
        self.sem = None
        self.val = None


class Prog:
    def __init__(self, nc):
        self.nc = nc
        self.ops = []
        self.last = {e: None for e in ENGS}
        self.dmas_since_bar = []
        self.bar = {e: set() for e in ENGS}

    def op(self, eng, emit, reads=(), writes=(), dma=False):
        o = Op(eng, emit, dma)
        reads = _flat(reads)
        writes = _flat(writes)
        idx = len(self.ops)
        o.idx = idx
        deps = o.deps
        for b in reads:
            if b.lw is not None:
                deps.add(b.lw)
        for b in writes:
            if b.lw is not None:
                deps.add(b.lw)
            deps.update(b.rd)
        if self.bar[eng]:
            deps.update(self.bar[eng])
            self.bar[eng] = set()
        if eng == "pe" and not dma:
            ops = self.ops
            o.deps = deps = {d for d in deps if ops[d].is_dma or ops[d].eng != "pe"}
        deps.discard(idx)
        ops_ = self.ops
        best = {}
        red = set()
        for d in deps:
            od = ops_[d]
            if od.is_dma:
                red.add(d)
            elif od.eng not in best or best[od.eng] < d:
                best[od.eng] = d
        red.update(best.values())
        o.deps = deps = red
        for b in reads:
            b.rd.append(idx)
        for b in writes:
            b.lw = idx
            b.rd = []
        self.ops.append(o)
        if dma:
            self.dmas_since_bar.append(idx)
        else:
            self.last[eng] = idx
        return o

    def barrier(self):
        s = set(self.dmas_since_bar)
        for e in ENGS:
            if self.last[e] is not None:
                s.add(self.last[e])
        self.dmas_since_bar = []
        for e in ENGS:
            self.bar[e] = set(s)

    def finalize(self, final_wait_eng="sp"):
        nc = self.nc
        ops = self.ops
        for o in ops:
            if o.is_dma:
                o.milestone = True
            for d in o.deps:
                ops[d].milestone = True
        self._stack = contextlib.ExitStack()
        st = self._stack
        eng_sem = {e: st.enter_context(nc.semaphore("s_" + e)) for e in ENGS}
        dma_sems = {e: [st.enter_context(nc.semaphore("d_%s_%d" % (e, i))) for i in range(n)]
                    for e, n in DMA_SLOTS.items()}
        eng_cnt = {e: 0 for e in ENGS}
        dma_cnt = {e: 0 for e in DMA_SLOTS}
        slot_uses = {e: [0] * n for e, n in DMA_SLOTS.items()}
        slot_prev = {e: [None] * n for e, n in DMA_SLOTS.items()}
        streams = {e: [] for e in ENGS}
        known = {e: {} for e in ENGS}

        def add_waits(e, plist):
            best = {}
            for p in plist:
                k = id(p.sem)
                if k not in best or best[k][1] < p.val:
                    best[k] = (p.sem, p.val)
            for k, (sem, val) in best.items():
                if known[e].get(k, 0) >= val:
                    continue
                known[e][k] = val
                streams[e].append(("wait", sem, val))

        last_dma = {}
        for o in ops:
            e = o.eng
            plist = [ops[d] for d in o.deps]
            if o.is_dma:
                n = DMA_SLOTS[e]
                i = dma_cnt[e] % n
                dma_cnt[e] += 1
                prev = slot_prev[e][i]
                if prev is not None:
                    plist.append(prev)
                add_waits(e, plist)
                slot_uses[e][i] += 1
                o.sem = dma_sems[e][i]
                o.val = 16 * slot_uses[e][i]
                slot_prev[e][i] = o
                last_dma[(e, i)] = o
                streams[e].append(("dma", o))
            else:
                add_waits(e, plist)
                if o.milestone:
                    eng_cnt[e] += 1
                    o.sem = eng_sem[e]
                    o.val = eng_cnt[e]
                streams[e].append(("op", o))
        add_waits(final_wait_eng, list(last_dma.values()))
        self.streams = streams
        self.stats = {e: len(streams[e]) for e in ENGS}
        self.max_sem = dict(eng_cnt)

    def emit(self):
        nc = self.nc
        streams = self.streams
        engmap = {"pe": "tensor", "act": "scalar", "dve": "vector", "pool": "gpsimd", "sp": "sync"}
        with nc.Block() as block:
            for e in ENGS:
                def body(eng, e=e):
                    for item in streams[e]:
                        if item[0] == "wait":
                            eng.wait_ge(item[1], item[2])
                        elif item[0] == "dma":
                            o = item[1]
                            o.emit(eng).then_inc(o.sem, 16)
                        else:
                            o = item[1]
                            ins = o.emit(eng)
                            if o.milestone:
                                ins.then_inc(o.sem, 1)
                getattr(block, engmap[e])(body)
        self._stack.close()


VC = {}
_off = 0
for _n, _w in [("g0", 8), ("g1", 8), ("g2", 8), ("g3", 8), ("g4", 8),
               ("cw0", 8), ("cw1", 8), ("cw2", 8), ("cw3", 8), ("cb", 8),
               ("ba_f", 8), ("bx_f", 8), ("lam_f", 8), ("ba_b", 8), ("bx_b", 8), ("lam_b", 8),
               ("flags", 4)]:
    VC[_n] = _off
    _off += _w
NV = _off

CH = {}
_c = 0
for _n, _k in [("in_xr", 8), ("in_gate", 8), ("out0", 8), ("mlp0", 64), ("q", 8), ("k", 8), ("v", 8),
               ("o", 8), ("mlp1", 64)]:
    CH[_n] = _c
    _c += _k
NCH = _c


def build_program(nseg, stop_after=None):
    NT = 2048 * nseg
    NTL = NT // T
    NP = NT // 128
    nc = bass.Bass("TRN2", target_bir_lowering=False)

    def din(name, shape, dt=F32):
        return nc.dram_tensor(name, list(shape), dt, kind="ExternalInput").ap()

    def dscr(name, shape, dt):
        return nc.dram_tensor(name, list(shape), dt, kind="Internal").ap()

    xc = din("xc", [NT, D])
    xhalo = din("xhalo", [NTL, 128, 24])
    vecs = din("vecs", [128, NV])
    rowmask = din("rowmask", [128, NP * 14])
    btab = din("btab", [128, 7 * 16 * 128])
    w_in = din("w_in", [D, 2 * D])
    w_out0 = din("w_out0", [D, D])
    w_up0 = din("w_up0", [D, 4 * D])
    w_dn0 = din("w_dn0", [4 * D, D])
    w_qkv = din("w_qkv", [D, 3 * D])
    w_o = din("w_o", [D, D])
    w_up1 = din("w_up1", [D, 4 * D])
    w_dn1 = din("w_dn1", [4 * D, D])
    gw = {k: din(k, [16, 64, 64]) for k in ("wa_f", "wx_f", "wa_b", "wx_b")}
    yc = nc.dram_tensor("yc", [NT, D], F32, kind="ExternalOutput").ap()

    WS = dscr("WS", [NCH, 128, 1024], BF16)
    HB_S = dscr("HB_S", [NTL, 128, 4096], BF16)
    X2_S = dscr("X2_S", [128, 8, NT], F32)
    QT_S = dscr("QT_S", [128, 8, NT], BF16)
    KT_S = dscr("KT_S", [128, 8, NT], BF16)
    V_S = dscr("V_S", [NP, 128, 1040], BF16)
    XT_S = dscr("XT_S", [128, 8, NT], F32)
    XN_S = dscr("XN_S", [128, 8, NT], BF16)
    XR_S = dscr("XR_S", [128, 8, NT], F32)

    st = contextlib.ExitStack()
    with st:
        NW = 52800
        SB = st.enter_context(nc.sbuf_tensor("arena", [128, NW], F32))
        PS = [st.enter_context(nc.psum_tensor("ps%d" % k, [128, 512], F32)) for k in range(8)]
        PSB = [Buf("ps%d" % k) for k in range(8)]
        P = Prog(nc)

        cur = [0]

        def alloc(words):
            o = cur[0]
            cur[0] += (words + 7) // 8 * 8
            assert cur[0] <= NW, ("SBUF arena overflow", cur[0])
            return o

        def f32v(off, n):
            return SB[:, off:off + n]

        def bf16v(off, nwords):
            return SB[:, off:off + nwords].bitcast(BF16)

        o_vecs = alloc(NV)
        VEC = f32v(o_vecs, NV)
        o_der = alloc(64)
        DER = f32v(o_der, 64)
        o_misc = alloc(16)
        MISC = f32v(o_misc, 16)
        o_carry = alloc(16)
        CARRY = f32v(o_carry, 16)
        IDF = f32v(alloc(128), 128)
        IDB = bf16v(alloc(64), 64)
        ONESB = bf16v(alloc(64), 64)
        o_rm = alloc(NP * 14)
        RM = f32v(o_rm, NP * 14)
        NSLOT = 8
        o_wr = alloc(NSLOT * 512)
        WR = [bf16v(o_wr + s * 512, 512).rearrange("p (k j) -> p k j", k=8) for s in range(NSLOT)]
        WRB = [Buf("wr%d" % s) for s in range(NSLOT)]
        o_xt = alloc(8 * 516)
        XT = f32v(o_xt, 8 * 516).rearrange("p (c t) -> p c t", c=8)
        o_xn = alloc(4 * 516)
        XN = bf16v(o_xn, 4 * 516).rearrange("p (c t) -> p c t", c=8)
        persist_end = cur[0]
        GWB = {k: bf16v(alloc(512), 512).rearrange("p (c j) -> p c j", c=8) for k in gw}
        gwb_end = cur[0]

        B_const = Buf("const")
        B_xt = Buf("xt")
        B_xn = Buf("xn")
        CUR = {"XT": XT, "XN": XN, "B_xt": B_xt, "B_xn": B_xn, "cast_eng": "pool", "b_eng": "pool"}
        B_carry = Buf("carry")
        WSB = [Buf("ws%d" % c) for c in range(NCH)]
        HBSB = [Buf("hbs%d" % i) for i in range(NTL)]
        X2SB = [Buf("x2s%d" % i) for i in range(NTL)]
        QSB = [Buf("qs%d" % i) for i in range(NTL)]
        KSB = [Buf("ks%d" % i) for i in range(NTL)]
        VSB = [Buf("vs%d" % i) for i in range(NTL)]
        XTSB = [Buf("xts%d" % i) for i in range(NTL)]
        XNSB = [Buf("xns%d" % i) for i in range(NTL)]
        XRSB = [Buf("xrs%d" % i) for i in range(NTL)]

        def MM(out, lhsT, rhs, start, stop, r, w):
            P.op("pe", lambda e: e.matmul(out, lhsT=lhsT, rhs=rhs, start=start, stop=stop), r, w)

        def TR(out, in_, ident, r, w):
            P.op("pe", lambda e: e.transpose(out=out, in_=in_, identity=ident), r, w)

        def ACT(out, in_, func, r, w, scale=1.0, bias=0.0):
            P.op("act", lambda e: e.activation(out=out, in_=in_, func=func, scale=scale, bias=bias), r, w)

        def TT(eng, out, in0, in1, op, r, w):
            P.op(eng, lambda e: e.tensor_tensor(out=out, in0=in0, in1=in1, op=op), r, w)

        def TS(eng, out, in0, s1, s2, op0, op1, r, w):
            if op1 is None:
                P.op(eng, lambda e: e.tensor_scalar(out=out, in0=in0, scalar1=s1, scalar2=None, op0=op0), r, w)
            else:
                P.op(eng, lambda e: e.tensor_scalar(out=out, in0=in0, scalar1=s1, scalar2=s2, op0=op0, op1=op1), r, w)

        def STT(out, in0, scalar, in1, op0, op1, r, w):
            P.op("dve", lambda e: e.scalar_tensor_tensor(out=out, in0=in0, scalar=scalar, in1=in1, op0=op0, op1=op1), r, w)

        def CP(eng, out, in_, r, w):
            if eng == "act":
                P.op("act", lambda e: e.activation(out=out, in_=in_, func=AF.Copy), r, w)
            else:
                P.op(eng, lambda e: e.tensor_copy(out=out, in_=in_), r, w)

        def MS(eng, ap, val, w):
            P.op(eng, lambda e: e.memset(ap, val), (), w)

        def DMA(q, out, in_, r, w):
            P.op(q, lambda e: e.dma_start(out=out, in_=in_), r, w, dma=True)

        bank_rr = [0]
        bank_pool = [list(range(6))]

        def newbank():
            lst = bank_pool[0]
            k = lst[bank_rr[0] % len(lst)]
            bank_rr[0] += 1
            return k

        wr_rr = [0]

        def wload(ch):
            s = wr_rr[0] % NSLOT
            wr_rr[0] += 1
            DMA("sp", WR[s].rearrange("p k j -> p (k j)"), WS[ch], [WSB[ch]], [WRB[s]])
            return WR[s], WRB[s]

        evac_rr = [0]

        def evac_eng():
            evac_rr[0] += 1
            return "act" if evac_rr[0] % 2 else "dve"

        DMA("sp", VEC, vecs, [], [B_const])
        DMA("sp", RM, rowmask, [], [B_const])
        MS("pool", MISC[:, 0:1], EPS, [B_const])
        MS("pool", MISC[:, 1:2], 1.0, [B_const])
        MS("pool", IDF, 0.0, [B_const])
        P.op("pool", lambda e: e.affine_select(out=IDF, in_=IDF, pattern=[[-1, 128]], compare_op=ALU.not_equal,
                                               fill=1.0, base=0, channel_multiplier=1), [B_const], [B_const])
        CP("pool", IDB, IDF, [B_const], [B_const])
        MS("pool", ONESB, 1.0, [B_const])
        EPSC = MISC[:, 0:1]
        ONEC = MISC[:, 1:2]

        def vcol(name, c=None):
            o = VC[name]
            if c is None:
                return VEC[:, o:o + 8]
            return VEC[:, o + c:o + c + 1]

        import os
        _skip = os.environ.get("KSKIP", "")
        DERI = {}
        for di, dn in enumerate(("f", "b")):
            base = di * 32
            hba = DER[:, base:base + 8]
            hbx = DER[:, base + 8:base + 16]
            cc = DER[:, base + 16:base + 24]
            hc = DER[:, base + 24:base + 32]
            if "der" in _skip:
                DERI[dn] = (base, base + 8, base + 16, base + 24)
                continue
            TS("dve", hba, vcol("ba_" + dn), 0.5, None, ALU.mult, None, [B_const], [B_const])
            TS("dve", hbx, vcol("bx_" + dn), 0.5, None, ALU.mult, None, [B_const], [B_const])
            ACT(cc, vcol("lam_" + dn), AF.Exp, [B_const], [B_const], scale=-1.0)
            ACT(cc, cc, AF.Ln, [B_const], [B_const], scale=1.0, bias=ONEC)
            TS("dve", hc, cc, -4.0, None, ALU.mult, None, [B_const], [B_const])
            TS("dve", cc, cc, -8.0, None, ALU.mult, None, [B_const], [B_const])
            DERI[dn] = (base, base + 8, base + 16, base + 24)

        def dcol(dn, which, c):
            o = DERI[dn][which] + c
            return DER[:, o:o + 1]

        cur[0] = gwb_end
        GST = f32v(alloc(1024), 1024).rearrange("p (c j) -> p c j", c=8)
        B_gst = Buf("gst")
        for k in ("wa_b", "wx_b", "wa_f", "wx_f"):
            if "gst" in _skip:
                break
            MS("pool", GST, 0.0, [B_gst])
            src = gw[k].rearrange("(c t) i j -> t i c j", t=2)
            DMA("sp", GST[0:64, :, 0:64], src[0], [], [B_gst])
            DMA("sp", GST[64:128, :, 64:128], src[1], [], [B_gst])
            CP("dve", GWB[k], GST, [B_gst], [B_const])

        conv_list = []

        def add_conv(name, W, row0, col0, n, stride_cols=128):
            for i in range(n):
                conv_list.append((CH[name] + i, W, row0, col0 + i * stride_cols))

        add_conv("in_xr", w_in, 0, 0, 8)
        add_conv("in_gate", w_in, 0, 1024, 8)
        add_conv("out0", w_out0, 0, 0, 8)
        for q in range(4):
            for f in range(8):
                conv_list.append((CH["mlp0"] + q * 16 + f, w_up0, 0, q * 1024 + f * 128))
            for oc in range(8):
                conv_list.append((CH["mlp0"] + q * 16 + 8 + oc, w_dn0, q * 1024, oc * 128))
        add_conv("q", w_qkv, 0, 0, 8)
        add_conv("k", w_qkv, 0, 1024, 8)
        add_conv("v", w_qkv, 0, 2048, 8)
        add_conv("o", w_o, 0, 0, 8)
        for q in range(4):
            for f in range(8):
                conv_list.append((CH["mlp1"] + q * 16 + f, w_up1, 0, q * 1024 + f * 128))
            for oc in range(8):
                conv_list.append((CH["mlp1"] + q * 16 + 8 + oc, w_dn1, q * 1024, oc * 128))
        conv_pos = [0]

        def emit_conv(n):
            for _ in range(n):
                if conv_pos[0] >= len(conv_list):
                    return
                ch, W, r0, c0 = conv_list[conv_pos[0]]
                conv_pos[0] += 1
                src = W[r0:r0 + 1024, c0:c0 + 128].rearrange("(kc p) j -> p kc j", p=128)
                dst = WS[ch].rearrange("p (kc j) -> p kc j", kc=8)
                DMA("pool", dst, src, [], [WSB[ch]])

        if "conv" not in _skip:
            emit_conv(8)
        P.barrier()
        if stop_after == "prologue":
            P.finalize(); P.emit()
            return nc, P

        PS7B = [Buf("ps7_%d" % c) for c in range(8)]
        PS7N = Buf("ps7n")

        def l0_scratch(tag):
            d = {}
            d["XIN"] = alloc(4096)
            for k in ("R1", "R2", "R3", "R4", "R5"):
                d[k] = alloc(8 * 516)
            d["XRB"] = alloc(2048)
            d["RS"] = alloc(520)
            d["XHT"] = alloc(24)
            B = {"XIN": Buf(tag + "xin"), "RS": Buf(tag + "rs"), "XHT": Buf(tag + "xht")}
            for k in ("R1", "R2", "R3", "R4", "R5", "XRB"):
                B[k] = [Buf("%s%s_%d" % (tag, k, c)) for c in range(8)]
            d["B"] = B
            for k in ("R1", "R2", "R3", "R4", "R5"):
                d["v" + k] = f32v(d[k], 8 * 516).rearrange("p (c t) -> p c t", c=8)
            d["vXRB"] = bf16v(d["XRB"], 2048).rearrange("p (c t) -> p c t", c=8)
            return d

        def rmsnorm(src, gname, ncols, out, out_buf, src_buf, SQ, B_sq, RS, B_rs):
            ACT(SQ[:, :, 0:ncols], src[:, :, 0:ncols], AF.Square, [src_buf], [B_sq])
            kb = newbank()
            for c in range(8):
                MM(PS[kb][:, 0:512], ONESB, SQ[:, c, 0:512], c == 0, c == 7, [B_sq, B_const], [PSB[kb]])
            ACT(RS[:, 0:512], PS[kb][:, 0:512], AF.Sqrt, [PSB[kb], B_const], [B_rs], scale=1.0 / D, bias=EPSC)
            if ncols > 512:
                for c in range(8):
                    MM(PS[7][:, 64:64 + ncols - 512], ONESB, SQ[:, c, 512:ncols], c == 0, c == 7, [B_sq, B_const], [PSB[7]])
                ACT(RS[:, 512:ncols], PS[7][:, 64:64 + ncols - 512], AF.Sqrt, [PSB[7], B_const], [B_rs],
                    scale=1.0 / D, bias=EPSC)
            P.op("dve", lambda e: e.reciprocal(out=RS[:, 0:ncols], in_=RS[:, 0:ncols]), [B_rs], [B_rs])
            for c in range(8):
                STT(out[:, c, 0:ncols], src[:, c, 0:ncols], vcol(gname, c), RS[:, 0:ncols], ALU.mult, ALU.mult,
                    [src_buf, B_rs, B_const], [out_buf])

        def rmsnorm_deferred(src, gname, out, out_buf, src_buf, SQ, B_sq, RS, B_rs):
            for c in range(8):
                if c < 4:
                    ACT(out[:, c, 0:512], src[:, c, 0:512], AF.Identity, [src_buf, B_const], [out_buf], scale=vcol(gname, c))
                else:
                    TS("dve", out[:, c, 0:512], src[:, c, 0:512], vcol(gname, c), None, ALU.mult, None,
                       [src_buf, B_const], [out_buf])
            ACT(SQ[:, :, 0:512], src[:, :, 0:512], AF.Square, [src_buf], [B_sq])
            kb = newbank()
            for c in range(8):
                MM(PS[kb][:, 0:512], ONESB, SQ[:, c, 0:512], c == 0, c == 7, [B_sq, B_const], [PSB[kb]])
            ACT(RS[:, 0:512], PS[kb][:, 0:512], AF.Sqrt, [PSB[kb], B_const], [B_rs], scale=1.0 / D, bias=EPSC)
            P.op("dve", lambda e: e.reciprocal(out=RS[:, 0:512], in_=RS[:, 0:512]), [B_rs], [B_rs])

        def l0_front_parts(i, L, SQ, B_sq):
            B = L["B"]
            XIN = [f32v(L["XIN"] + b * 1024, 1024) for b in range(4)]
            XHT = f32v(L["XHT"], 24).rearrange("p (c m) -> p c m", c=8)
            RS = f32v(L["RS"], 520)
            t0 = i * T
            ncols = 515

            def fa(part):
                if part == 0:
                    for b in range(4):
                        DMA("sp", XIN[b], xc[t0 + b * 128:t0 + (b + 1) * 128, :], [], [B["XIN"]])
                    DMA("sp", XHT.rearrange("p c m -> p (c m)"), xhalo[i], [], [B["XHT"]])
                for c in range(2 * part, 2 * part + 2):
                    kb = newbank()
                    for b in range(4):
                        TR(PS[kb][:, b * 128:(b + 1) * 128], XIN[b][:, c * 128:(c + 1) * 128], IDF, [B["XIN"], B_const], [PSB[kb]])
                    CP("act", XT[:, c, 0:512], PS[kb][:, 0:512], [PSB[kb]], [B_xt])
                if part == 3:
                    CP("act", XT[:, :, 512:515], XHT, [B["XHT"]], [B_xt])

            def fb():
                ACT(SQ[:, :, 0:ncols], XT[:, :, 0:ncols], AF.Square, [B_xt], [B_sq])
                kb = newbank()
                for c in range(8):
                    MM(PS[kb][:, 0:512], ONESB, SQ[:, c, 0:512], c == 0, c == 7, [B_sq, B_const], [PSB[kb]])
                ACT(RS[:, 0:512], PS[kb][:, 0:512], AF.Sqrt, [PSB[kb], B_const], [B["RS"]], scale=1.0 / D, bias=EPSC)
                for c in range(8):
                    MM(PS[7][:, 64:64 + ncols - 512], ONESB, SQ[:, c, 512:ncols], c == 0, c == 7, [B_sq, B_const], [PSB[7]])
                ACT(RS[:, 512:ncols], PS[7][:, 64:64 + ncols - 512], AF.Sqrt, [PSB[7], B_const], [B["RS"]],
                    scale=1.0 / D, bias=EPSC)
                P.op("dve", lambda e: e.reciprocal(out=RS[:, 0:ncols], in_=RS[:, 0:ncols]), [B["RS"]], [B["RS"]])

            def fc():
                for c in range(8):
                    STT(XN[:, c, 0:ncols], XT[:, c, 0:ncols], vcol("g0", c), RS[:, 0:ncols], ALU.mult, ALU.mult,
                        [B_xt, B["RS"], B_const], [B_xn])
            return [lambda: fa(0), lambda: fa(1), lambda: fa(2), lambda: fa(3), fb, fc]

        def l0_gate_branch(L, G, BG):
            B = L["B"]
            XG, T3, T4 = L["vR5"], L["vR3"], L["vR4"]

            def s0(c):
                w, wb = wload(CH["in_gate"] + c)
                kb = newbank()
                for kc in range(8):
                    MM(PS[kb][:, 0:512], w[:, kc, :], CUR["XN"][:, kc, 0:512], kc == 0, kc == 7, [wb, CUR["B_xn"]], [PSB[kb]])
                CP("act", XG[:, c, 0:512], PS[kb][:, 0:512], [PSB[kb]], [B["R5"][c]])
                ACT(T4[:, c, 0:512], XG[:, c, 0:512], AF.Square, [B["R5"][c]], [B["R4"][c]])
                TS("dve", T4[:, c, 0:512], T4[:, c, 0:512], GC1, 1.0, ALU.mult, ALU.add, [B["R4"][c]], [B["R4"][c]])
                TT("pool", T4[:, c, 0:512], T4[:, c, 0:512], XG[:, c, 0:512], ALU.mult, [B["R4"][c], B["R5"][c]], [B["R4"][c]])

            def s1(c):
                ACT(T3[:, c, 0:512], T4[:, c, 0:512], AF.Tanh, [B["R4"][c]], [B["R3"][c]], scale=GC0)
                STT(G[:, c, :], T3[:, c, 0:512], 1.0, XG[:, c, 0:512], ALU.add, ALU.mult, [B["R3"][c], B["R5"][c]], [BG[c]])

            for k in range(9):
                if k < 8:
                    s0(k)
                if k >= 1:
                    s1(k - 1)

        def l0_xr_a(dn, L, have_xr=False, gate=None, hooks=None, pre_hooks=None):
            B = L["B"]
            U, XR, RT, IT, M, XRB = L["vR1"], L["vR2"], L["vR3"], L["vR4"], L["vR5"], L["vXRB"]
            A = U
            ga, gx = GWB["wa_" + dn], GWB["wx_" + dn]

            def s0(c):
                if have_xr:
                    CP("pool", XRB[:, c, :], XR[:, c, 0:512], [B["R2"][c]], [B["XRB"][c]])
                    return
                w, wb = wload(CH["in_xr"] + c)
                kb = newbank()
                for kc in range(8):
                    MM(PS[kb][:, 0:512], w[:, kc, :], CUR["XN"][:, kc, 0:512], kc == 0, kc == 7, [wb, CUR["B_xn"]], [PSB[kb]])
                hb = 6 + (c % 2)
                for kc in range(8):
                    MM(PS[hb][:, c * 4:c * 4 + 3], w[:, kc, :], CUR["XN"][:, kc, 512:515], kc == 0, kc == 7, [wb, CUR["B_xn"]], [PSB[hb]])
                CP("act", U[:, c, 2:514], PS[kb][:, 0:512], [PSB[kb]], [B["R1"][c]])
                CP("dve", U[:, c, 0:2], PS[hb][:, c * 4:c * 4 + 2], [PSB[hb]], [B["R1"][c]])
                CP("dve", U[:, c, 514:515], PS[hb][:, c * 4 + 2:c * 4 + 3], [PSB[hb]], [B["R1"][c]])
                TS("dve", XR[:, c, 0:512], U[:, c, 0:512], vcol("cw0", c), vcol("cb", c), ALU.mult, ALU.add,
                   [B["R1"][c], B_const], [B["R2"][c]])
                for k in (1, 2, 3):
                    STT(XR[:, c, 0:512], U[:, c, k:k + 512], vcol("cw%d" % k, c), XR[:, c, 0:512], ALU.mult, ALU.add,
                        [B["R1"][c], B["R2"][c], B_const], [B["R2"][c]])

            def s0b(c):
                CP(CUR["cast_eng"], XRB[:, c, :], XR[:, c, 0:512], [B["R2"][c]], [B["XRB"][c]])

            def s1(c):
                k1 = newbank()
                MM(PS[k1][:, 0:512], ga[:, c, :], XRB[:, c, :], True, True, [B["XRB"][c], B_const], [PSB[k1]])
                k2 = newbank()
                MM(PS[k2][:, 0:512], gx[:, c, :], XRB[:, c, :], True, True, [B["XRB"][c], B_const], [PSB[k2]])
                ACT(RT[:, c, 0:512], PS[k1][:, 0:512], AF.Tanh, [PSB[k1], B_const], [B["R3"][c]], scale=0.5, bias=dcol(dn, 0, c))
                ACT(IT[:, c, 0:512], PS[k2][:, 0:512], AF.Tanh, [PSB[k2], B_const], [B["R4"][c]], scale=0.5, bias=dcol(dn, 1, c))
                ACT(A[:, c, 0:512], RT[:, c, 0:512], AF.Exp, [B["R3"][c], B_const], [B["R1"][c]],
                    scale=dcol(dn, 3, c), bias=dcol(dn, 3, c))
                ACT(M[:, c, 0:512], RT[:, c, 0:512], AF.Exp, [B["R3"][c], B_const], [B["R5"][c]],
                    scale=dcol(dn, 2, c), bias=dcol(dn, 2, c))
                STT(IT[:, c, 0:512], IT[:, c, 0:512], 1.0, XR[:, c, 0:512], ALU.add, ALU.mult, [B["R4"][c], B["R2"][c]], [B["R4"][c]])

            LAG = 2
            if gate is not None:
                G_, BG_ = gate
                XGv = U
                gs = {}

                def g0(c):
                    w, wb = wload(CH["in_gate"] + c)
                    kb = newbank()
                    for kc in range(8):
                        MM(PS[kb][:, 0:512], w[:, kc, :], CUR["XN"][:, kc, 0:512], kc == 0, kc == 7, [wb, CUR["B_xn"]], [PSB[kb]])
                    gs[c] = kb

                def g1(c):
                    kb = gs[c]
                    CP("dve", XGv[:, c, 0:512], PS[kb][:, 0:512], [PSB[kb]], [B["R1"][c]])
                    ACT(RT[:, c, 0:512], XGv[:, c, 0:512], AF.Square, [B["R1"][c]], [B["R3"][c]])
                    TS("dve", RT[:, c, 0:512], RT[:, c, 0:512], GC1, 1.0, ALU.mult, ALU.add, [B["R3"][c]], [B["R3"][c]])
                    TT("pool", RT[:, c, 0:512], RT[:, c, 0:512], XGv[:, c, 0:512], ALU.mult, [B["R3"][c], B["R1"][c]], [B["R3"][c]])

                def g2(c):
                    ACT(RT[:, c, 0:512], RT[:, c, 0:512], AF.Tanh, [B["R3"][c]], [B["R3"][c]], scale=GC0)
                    STT(G_[:, c, :], RT[:, c, 0:512], 1.0, XGv[:, c, 0:512], ALU.add, ALU.mult, [B["R3"][c], B["R1"][c]], [BG_[c]])
                for k in range(8 + 3):
                    if k < 8:
                        g0(k)
                        s0(k)
                    if 1 <= k < 9:
                        g1(k - 1)
                    if 2 <= k < 10:
                        g2(k - 2)
                    if k >= 3:
                        s1(k - 3)
            else:
                for k in range(8 + 3):
                    if pre_hooks and k in pre_hooks:
                        pre_hooks[k]()
                    if k < 8:
                        s0(k)
                    if 1 <= k < 9:
                        s0b(k - 1)
                    if k >= 3:
                        s1(k - 3)
                    if hooks and k in hooks:
                        hooks[k]()
            for hf in range(2):
                ACT(M[:, 4 * hf:4 * hf + 4, 0:512], M[:, 4 * hf:4 * hf + 4, 0:512], AF.Sqrt,
                    [B["R5"][4 * hf:4 * hf + 4], B_const], [B["R5"][4 * hf:4 * hf + 4]], scale=-1.0, bias=ONEC)

        def l0_xr_b(L, first, flag_col, reverse, post, post_b=None):
            B = L["B"]
            U, RT, IT, M = L["vR1"], L["vR3"], L["vR4"], L["vR5"]
            A, H = U, RT
            if not first and flag_col is not None:
                TS("dve", CARRY[:, 0:8], CARRY[:, 0:8], flag_col, None, ALU.mult, None, [B_carry, B_const], [B_carry])
            for c in range(8):
                TT(CUR["b_eng"], IT[:, c, 0:512], IT[:, c, 0:512], M[:, c, 0:512], ALU.mult, [B["R4"][c], B["R5"][c]], [B["R4"][c]])
            for c in range(8):
                init = 0.0 if first else CARRY[:, c:c + 1]
                if reverse:
                    o_, a_, b_ = H[:, c, 511::-1], A[:, c, 511::-1], IT[:, c, 511::-1]
                else:
                    o_, a_, b_ = H[:, c, 0:512], A[:, c, 0:512], IT[:, c, 0:512]
                P.op("dve", (lambda o_, a_, b_, init: (lambda e: e.tensor_tensor_scan(
                    out=o_, data0=a_, data1=b_, initial=init, op0=ALU.mult, op1=ALU.add)))(o_, a_, b_, init),
                    [B["R1"][c], B["R4"][c], B_carry], [B["R3"][c]])
                col = 0 if reverse else 511
                CP("dve", CARRY[:, c:c + 1], H[:, c, col:col + 1], [B["R3"][c]], [B_carry])
                post(c)
                if post_b is not None and c >= 1:
                    post_b(c - 1)
            if post_b is not None:
                post_b(7)

        def l0_xr_b_units(L, first, flag_col, reverse, post):
            B = L["B"]
            U, RT, IT, M = L["vR1"], L["vR3"], L["vR4"], L["vR5"]
            A, H = U, RT

            def unit(c):
                if c == 0 and (not first) and flag_col is not None:
                    TS("dve", CARRY[:, 0:8], CARRY[:, 0:8], flag_col, None, ALU.mult, None, [B_carry, B_const], [B_carry])
                TT("dve", IT[:, c, 0:512], IT[:, c, 0:512], M[:, c, 0:512], ALU.mult, [B["R4"][c], B["R5"][c]], [B["R4"][c]])
                init = 0.0 if first else CARRY[:, c:c + 1]
                if reverse:
                    o_, a_, b_ = H[:, c, 511::-1], A[:, c, 511::-1], IT[:, c, 511::-1]
                else:
                    o_, a_, b_ = H[:, c, 0:512], A[:, c, 0:512], IT[:, c, 0:512]
                P.op("dve", (lambda o_, a_, b_, init: (lambda e: e.tensor_tensor_scan(
                    out=o_, data0=a_, data1=b_, initial=init, op0=ALU.mult, op1=ALU.add)))(o_, a_, b_, init),
                    [B["R1"][c], B["R4"][c], B_carry], [B["R3"][c]])
                col = 0 if reverse else 511
                CP("dve", CARRY[:, c:c + 1], H[:, c, col:col + 1], [B["R3"][c]], [B_carry])
                post(c)
            return [(lambda c=c: unit(c)) for c in range(8)]

        def proj_fm(chbase, rhs_fn, rhs_bufs, consume):
            for oc in range(8):
                w, wb = wload(chbase + oc)
                kb = newbank()
                for kc in range(8):
                    MM(PS[kb][:, 0:512], w[:, kc, :], rhs_fn(kc), kc == 0, kc == 7, [wb] + rhs_bufs, [PSB[kb]])
                consume(oc, kb, w, wb)

        def mlp_gen(chbase, xres, xres_buf, HID, HIDB, TMP, TMPB, RS=None, B_rs=None):
            def up(q):
                for f in range(8):
                    w, wb = wload(chbase + q * 16 + f)
                    kb = newbank()
                    for kc in range(8):
                        MM(PS[kb][:, 0:512], w[:, kc, :], CUR["XN"][:, kc, 0:512], kc == 0, kc == 7, [wb, CUR["B_xn"]], [PSB[kb]])
                    t = f % 2
                    if RS is None:
                        ACT(TMP[t], PS[kb][:, 0:512], AF.Relu, [PSB[kb]], [TMPB[t]])
                    else:
                        STT(TMP[t], PS[kb][:, 0:512], 0.0, RS[:, 0:512], ALU.max, ALU.mult, [PSB[kb], B_rs], [TMPB[t]])
                        ACT(HID[q % 2][:, f, :], TMP[t], AF.Square, [TMPB[t]], [HIDB[q % 2]])
                        yield
                        continue
                    TT("pool", HID[q % 2][:, f, :], TMP[t], TMP[t], ALU.mult, [TMPB[t]], [HIDB[q % 2]])
                    yield

            def down(q):
                for oc in range(8):
                    w, wb = wload(chbase + q * 16 + 8 + oc)
                    kb = newbank()
                    for f in range(8):
                        MM(PS[kb][:, 0:512], w[:, f, :], HID[q % 2][:, f, :], f == 0, f == 7, [wb, HIDB[q % 2]], [PSB[kb]])
                    TT("dve", xres[:, oc, 0:512], PS[kb][:, 0:512], xres[:, oc, 0:512], ALU.add,
                       [PSB[kb], xres_buf], [xres_buf])
                    yield

            yield from up(0)
            for q in range(4):
                if q + 1 < 4:
                    yield from up(q + 1)
                yield from down(q)

        def mlp(*a):
            for _ in mlp_gen(*a):
                pass

        def seg_flag(tile_start_tok):
            seg = tile_start_tok // 2048
            return VEC[:, VC["flags"] + seg:VC["flags"] + seg + 1]

        cur[0] = gwb_end
        L = l0_scratch("p1")
        HBO = bf16v(alloc(2048), 2048).rearrange("p (c t) -> p c t", c=8)
        B_hbo = Buf("hbo")
        bank_pool[0] = list(range(6))
        per_tile_conv = (len(conv_list) - 8 + NTL - 1) // NTL
        if "convfirst" in _skip:
            emit_conv(len(conv_list))
        SQ1 = bf16v(alloc(4 * 516), 4 * 516).rearrange("p (c t) -> p c t", c=8)
        B_sq1 = Buf("sq1")
        order = list(range(NTL - 1, -1, -1))

        CUR.update({"cast_eng": "act", "b_eng": "dve"})
        c3 = (per_tile_conv + 2) // 3

        def front1_parts(i):
            parts = l0_front_parts(i, L, SQ1, B_sq1)

            def stores():
                DMA("pool", XT_S[:, :, i * T:(i + 1) * T], XT[:, :, 0:512], [B_xt], [XTSB[i]])
                DMA("pool", XN_S[:, :, i * T:(i + 1) * T], XN[:, :, 0:512], [B_xn], [XNSB[i]])
                emit_conv(c3)
            return parts, stores
        parts0, stores0 = front1_parts(order[0])
        for f in parts0:
            f()
        stores0()
        _maxn = int(os.environ.get("KMAXN", "999"))
        pending = None
        for n, i in enumerate(order):
            if n >= _maxn:
                break
            first = (n == 0)
            flag_col = None
            if not first and ((i + 1) * T) % 2048 == 0:
                flag_col = seg_flag((i + 1) * T)
            if n + 1 < NTL:
                nparts, nstores = front1_parts(order[n + 1])
                hooks = {1: nparts[0], 2: nparts[1], 3: nparts[2], 4: nparts[3], 6: nparts[4]}
            else:
                nparts, nstores, hooks = None, None, None
            pre = None
            if pending is not None:
                pre = {k: pending[0][k] for k in range(8)}
                pre[8] = pending[1]
            l0_xr_a("b", L, hooks=hooks, pre_hooks=pre)
            DMA("pool", XR_S[:, :, i * T:(i + 1) * T], L["vR2"][:, :, 0:512], [L["B"]["R2"]], [XRSB[i]])
            emit_conv(c3)
            if nparts is not None:
                nparts[5]()
                nstores()

            def post1(c, L=L):
                CP("act", HBO[:, c, :], L["vR3"][:, c, 0:512], [L["B"]["R3"][c]], [B_hbo])

            def fin1(i=i):
                DMA("pool", HB_S[i], HBO.rearrange("p c t -> p (c t)"), [B_hbo], [HBSB[i]])
                emit_conv(c3)
            pending = (l0_xr_b_units(L, first, flag_col, True, post1), fin1)
        if pending is not None:
            for u in pending[0]:
                u()
            pending[1]()
        emit_conv(len(conv_list))
        CUR.update({"cast_eng": "pool", "b_eng": "dve"})
        P.barrier()
        if stop_after == "pass1":
            P.finalize(); P.emit()
            return nc, P

        cur[0] = gwb_end
        bank_pool[0] = list(range(8))
        L = l0_scratch("p2")
        B = L["B"]
        G = bf16v(alloc(2048), 2048).rearrange("p (c t) -> p c t", c=8)
        HBL = bf16v(alloc(2048), 2048).rearrange("p (c t) -> p c t", c=8)
        TMP = [f32v(alloc(512), 512) for _ in range(2)]
        XT_R = [XT, f32v(L["XIN"], 4096).rearrange("p (c t) -> p c t", c=8)]
        XN_R = [XN, bf16v(alloc(2048), 2048).rearrange("p (c t) -> p c t", c=8)]
        BXT_R = [B_xt, Buf("xt1")]
        BXN_R = [B_xn, Buf("xn1")]
        BG = [Buf("g%d" % c) for c in range(8)]
        B_hbl = Buf("hbl")
        TMPB = [Buf("tmp0"), Buf("tmp1")]
        HID = [bf16v(L["R2"] + h * 2064, 2048).rearrange("p (c t) -> p c t", c=8) for h in range(2)]
        HIDB = [B["R2"][0:4], B["R2"][4:8]]
        QT = bf16v(L["R5"], 2048).rearrange("p (c t) -> p c t", c=8)
        KT = bf16v(L["R5"] + 2064, 2048).rearrange("p (c t) -> p c t", c=8)
        VST = bf16v(L["R4"], 4 * 520).rearrange("p (j c) -> p j c", j=4)
        BVST = B["R4"][0:5]
        Y = L["vXRB"]
        SQ2 = bf16v(L["R1"], 4 * 516).rearrange("p (c t) -> p c t", c=8)
        RS2 = f32v(L["RS"], 520)
        RC4 = f32v(alloc(8), 8)
        B_rc4 = Buf("rc4")

        def loads2(i):
            r = i % 2
            DMA("sp", XT_R[r][:, :, 0:512], XT_S[:, :, i * T:(i + 1) * T], [XTSB[i]], [BXT_R[r]])
            DMA("sp", XN_R[r][:, :, 0:512], XN_S[:, :, i * T:(i + 1) * T], [XNSB[i]], [BXN_R[r]])
            DMA("sp", HBL.rearrange("p c t -> p (c t)"), HB_S[i], [HBSB[i]], [B_hbl])

        loads2(0)
        for i in range(NTL):
            first = (i == 0)
            flag_col = None
            if not first and (i * T) % 2048 == 0:
                flag_col = seg_flag(i * T)
            r = i % 2
            XTc, XNc, BXT, BXN = XT_R[r], XN_R[r], BXT_R[r], BXN_R[r]
            CUR.update({"XT": XTc, "XN": XNc, "B_xt": BXT, "B_xn": BXN})
            DMA("sp", L["vR2"][:, :, 0:512], XR_S[:, :, i * T:(i + 1) * T], [XRSB[i]], [B["R2"]])

            def post2a(c):
                Hc = L["vR3"][:, c, 0:512]
                TT("pool", Hc, Hc, HBL[:, c, :], ALU.add, [B["R3"][c], B_hbl], [B["R3"][c]])

            def post2b(c):
                Hc = L["vR3"][:, c, 0:512]
                STT(Y[:, c, :], Hc, 0.25, G[:, c, :], ALU.mult, ALU.mult, [B["R3"][c], BG[c]], [B["XRB"][c]])
            l0_xr_a("f", L, have_xr=True, gate=(G, BG))
            ows = [wload(CH["out0"] + oc) for oc in range(8)]
            okb = [newbank() for _ in range(8)]

            def post2c(c):
                post2b(c)
                for oc in range(8):
                    MM(PS[okb[oc]][:, 0:512], ows[oc][0][:, c, :], Y[:, c, :], c == 0, c == 7,
                       [ows[oc][1], B["XRB"][c]], [PSB[okb[oc]]])
            l0_xr_b(L, first, flag_col, False, post2a, post2c)
            for oc in range(8):
                TT("dve", XTc[:, oc, 0:512], PS[okb[oc]][:, 0:512], XTc[:, oc, 0:512], ALU.add, [PSB[okb[oc]], BXT], [BXT])
            if i + 1 < NTL:
                loads2(i + 1)
            rmsnorm_deferred(XTc, "g1", XNc, BXN, BXT, SQ2, B["R1"][0:4], RS2, B["RS"])
            mlp(CH["mlp0"], XTc, BXT, HID, HIDB, TMP, TMPB, RS2, B["RS"])
            DMA("pool", X2_S[:, :, i * T:(i + 1) * T], XTc[:, :, 0:512], [BXT], [X2SB[i]])
            rmsnorm_deferred(XTc, "g2", XNc, BXN, BXT, SQ2, B["R1"][0:4], RS2, B["RS"])
            kbv = newbank()
            for pr in range(4):
                for c in range(8):
                    MM(PS[kbv][:, pr:pr + 1], SQ2[:, c, pr * 128:(pr + 1) * 128], ONESB[:, 0:1], c == 0, c == 7,
                       [B["R1"][0:4], B_const], [PSB[kbv]])
            ACT(RC4[:, 0:4], PS[kbv][:, 0:4], AF.Sqrt, [PSB[kbv], B_const], [B_rc4], scale=1.0 / D, bias=EPSC)
            P.op("dve", lambda e: e.reciprocal(out=RC4[:, 0:4], in_=RC4[:, 0:4]), [B_rc4], [B_rc4])

            def cons_q(oc, kb, w, wb):
                TT("dve", QT[:, oc, :], PS[kb][:, 0:512], RS2[:, 0:512], ALU.mult, [PSB[kb], B["RS"]], [B["R5"][0:4]])

            def cons_k(oc, kb, w, wb):
                TT("dve", KT[:, oc, :], PS[kb][:, 0:512], RS2[:, 0:512], ALU.mult, [PSB[kb], B["RS"]], [B["R5"][4:8]])
            proj_fm(CH["q"], lambda kc: XNc[:, kc, 0:512], [BXN], cons_q)
            proj_fm(CH["k"], lambda kc: XNc[:, kc, 0:512], [BXN], cons_k)
            MS("pool", VST, 1.0, [BVST])
            for vg in range(2):
                kbs = [newbank() for _ in range(4)]
                for vcl in range(4):
                    w, wb = wload(CH["v"] + vg * 4 + vcl)
                    for pr in range(4):
                        for kc in range(8):
                            MM(PS[kbs[pr]][:, vcl * 128:(vcl + 1) * 128], XNc[:, kc, pr * 128:(pr + 1) * 128], w[:, kc, :],
                               kc == 0, kc == 7, [wb, BXN], [PSB[kbs[pr]]])
                for pr in range(4):
                    dst = VST[:, pr, :].rearrange("p (h c) -> p h c", c=65)[:, vg * 8:(vg + 1) * 8, 0:64]
                    srcv = PS[kbs[pr]][:, 0:512].rearrange("p (h c) -> p h c", c=64)
                    if pr % 2 == 0:
                        ACT(dst, srcv, AF.Identity, [PSB[kbs[pr]], B_rc4], [BVST], scale=RC4[:, pr:pr + 1])
                    else:
                        TS("dve", dst, srcv, RC4[:, pr:pr + 1], None, ALU.mult, None, [PSB[kbs[pr]], B_rc4], [BVST])
            DMA("pool", QT_S[:, :, i * T:(i + 1) * T], QT, [B["R5"][0:4]], [QSB[i]])
            DMA("pool", KT_S[:, :, i * T:(i + 1) * T], KT, [B["R5"][4:8]], [KSB[i]])
            DMA("pool", V_S[4 * i:4 * i + 4].rearrange("j p c -> p j c"), VST, [BVST], [VSB[i]])
        CUR.update({"XT": XT, "XN": XN, "B_xt": B_xt, "B_xn": B_xn})
        P.barrier()
        if stop_after == "pass2":
            P.finalize(); P.emit()
            return nc, P

        cur[0] = persist_end
        bank_pool[0] = list(range(8))
        QT3 = [bf16v(alloc(2048), 2048).rearrange("p (c t) -> p c t", c=8) for _ in range(2)]
        KTW = bf16v(alloc(5120), 5120).rearrange("p (c t) -> p c t", c=8)
        VW = bf16v(alloc(5200), 5200).rearrange("p (s c) -> p s c", s=10)
        BT = bf16v(alloc(7168), 7168).rearrange("p (o h q) -> p o h q", o=7, h=16)
        PB = [[bf16v(alloc(256), 256) for _ in range(6)] for _ in range(2)]
        AO = [bf16v(alloc(512), 512) for _ in range(4)]
        AT = bf16v(alloc(2048), 2048).rearrange("p (c t) -> p c t", c=8)
        HID3 = [bf16v(alloc(2048), 2048).rearrange("p (c t) -> p c t", c=8) for _ in range(2)]
        TMP3 = [f32v(alloc(512), 512) for _ in range(2)]
        RC = f32v(alloc(8), 8)
        RS3 = f32v(alloc(520), 520)
        SQ3 = bf16v(alloc(2048), 2048).rearrange("p (c t) -> p c t", c=8)
        YTB = [f32v(alloc(1024), 1024).rearrange("p (c t) -> p c t", c=8) for _ in range(2)]
        OUTS = [f32v(alloc(1024), 1024) for _ in range(2)]
        B_q3, B_kw, B_vw, B_bt = Buf("q3"), Buf("kw"), Buf("vw"), Buf("bt")
        B_pb = [Buf("pb0"), Buf("pb1")]
        B_ao = [Buf("ao%d" % k) for k in range(4)]
        B_at, B_rc, B_rs3, B_sq3 = Buf("at"), Buf("rc"), Buf("rs3"), Buf("sq3")
        B_ytb = [Buf("ytb0"), Buf("ytb1")]
        B_outs = [Buf("outs0"), Buf("outs1")]
        B_hid3 = [Buf("hid3a"), Buf("hid3b")]
        B_tmp3 = [Buf("tmp3a"), Buf("tmp3b")]
        DMA("pool", BT.rearrange("p o h q -> p (o h q)"), btab, [], [B_bt])
        MS("pool", QT3[0][64:128], 0.0, [B_q3])
        MS("pool", QT3[1][0:64], 0.0, [B_q3])

        def attn_loads(i):
            t0 = i * T
            DMA("sp", QT3[0][0:64], QT_S[0:64, :, t0:t0 + T], [QSB[i]], [B_q3])
            DMA("sp", QT3[1][64:128], QT_S[64:128, :, t0:t0 + T], [QSB[i]], [B_q3])
            p_lo = 4 * i - 3
            s0 = max(0, -p_lo)
            s1 = min(10, NP - p_lo)
            if s0 > 0:
                MS("pool", KTW[:, :, 0:s0 * 128], 0.0, [B_kw])
                MS("pool", VW[:, 0:s0, :], 0.0, [B_vw])
            if s1 < 10:
                MS("pool", KTW[:, :, s1 * 128:1280], 0.0, [B_kw])
                MS("pool", VW[:, s1:10, :], 0.0, [B_vw])
            tiles_touched = sorted(set((p_lo + s_) // 4 for s_ in range(s0, s1)))
            DMA("sp", KTW[:, :, s0 * 128:s1 * 128], KT_S[:, :, (p_lo + s0) * 128:(p_lo + s1) * 128],
                [KSB[t] for t in tiles_touched], [B_kw])
            DMA("sp", VW[:, s0:s1, :], V_S[p_lo + s0:p_lo + s1].rearrange("j p c -> p j c"),
                [VSB[t] for t in tiles_touched], [B_vw])

        def attn_units(i):
            p_lo = 4 * i - 3
            state = {"pend": None}

            def emit_pv(jl, hg, olist, par):
                ko = newbank()
                Ov = PS[ko][:, 0:260].rearrange("p (h c) -> p h c", c=65)
                for hh in range(4):
                    h = hg * 4 + hh
                    for oi, o in enumerate(olist):
                        slot = 4 * i + jl + o - p_lo
                        MM(Ov[:, hh, :], PB[par][oi][:, hh * 128:(hh + 1) * 128], VW[:, slot, h * 65:(h + 1) * 65],
                           oi == 0, oi == len(olist) - 1, [B_pb[par], B_vw], [PSB[ko]])
                P.op("dve", lambda e: e.reciprocal(out=RC[:, 0:4], in_=Ov[:, :, 64]), [PSB[ko]], [B_rc])
                dst = AO[jl].rearrange("p (h c) -> p h c", c=64)[:, hg * 4:(hg + 1) * 4, :]
                TT("dve", dst, Ov[:, :, 0:64], RC[:, 0:4].unsqueeze(2).to_broadcast([128, 4, 64]), ALU.mult,
                   [PSB[ko], B_rc], [B_ao[jl]])

            def emit_tr(jl):
                kb = newbank()
                pv = PS[kb][:].bitcast(BF16)
                for kc in range(8):
                    TR(pv[:, kc * 128:(kc + 1) * 128], AO[jl][:, kc * 128:(kc + 1) * 128], IDB, [B_ao[jl], B_const], [PSB[kb]])
                CP("act", AT[:, :, jl * 128:(jl + 1) * 128], pv.rearrange("p (c t) -> p c t", c=8), [PSB[kb]], [B_at])

            def flush():
                pend = state["pend"]
                if pend is not None:
                    emit_pv(*pend)
                    if pend[1] == 3:
                        emit_tr(pend[0])
                state["pend"] = None

            def step(sidx):
                jl, hg = sidx // 4, sidx % 4
                j = 4 * i + jl
                olist = [-2, -1, 0, 1, 2]
                if j % 16 == 15:
                    olist = [-3] + olist
                if j % 16 == 0:
                    olist = olist + [3]
                interior = (j % 16) not in (0, 1, 14, 15)
                par = sidx % 2
                for oi, o in enumerate(olist):
                    slot = j + o - p_lo
                    ks = newbank()
                    for hh in range(4):
                        h = hg * 4 + hh
                        MM(PS[ks][:, hh * 128:(hh + 1) * 128], KTW[:, h // 2, slot * 128:(slot + 1) * 128],
                           QT3[h % 2][:, h // 2, jl * 128:(jl + 1) * 128], True, True, [B_kw, B_q3], [PSB[ks]])
                    Sv = PS[ks][:, 0:512].rearrange("p (h q) -> p h q", h=4)
                    STT(Sv, Sv, 0.125, BT[:, o + 3, hg * 4:(hg + 1) * 4, :], ALU.mult, ALU.add, [PSB[ks], B_bt], [PSB[ks]])
                    if interior and o in (-1, 0, 1):
                        ACT(PB[par][oi], PS[ks][:, 0:512], AF.Exp, [PSB[ks]], [B_pb[par]], scale=1.0)
                    else:
                        for a in range(2):
                            colx = (j * 7 + (o + 3)) * 2 + a
                            src = PS[ks][:, 0:512].rearrange("p (h a c) -> p h a c", h=4, a=2)[:, :, a, :]
                            dst = PB[par][oi].rearrange("p (h a c) -> p h a c", h=4, a=2)[:, :, a, :]
                            ACT(dst, src, AF.Exp, [PSB[ks], B_const], [B_pb[par]], scale=1.0, bias=RM[:, colx:colx + 1])
                flush()
                state["pend"] = (jl, hg, olist, par)

            units = [(lambda sidx=sidx: step(sidx)) for sidx in range(16)]
            units.append(flush)
            return units

        attn_loads(0)
        for u in attn_units(0):
            u()
        for i in range(NTL):
            t0 = i * T
            DMA("sp", XT[:, :, 0:512], X2_S[:, :, t0:t0 + T], [X2SB[i]], [B_xt])

            def cons_res3(oc, kb, w, wb):
                TT("dve", XT[:, oc, 0:512], PS[kb][:, 0:512], XT[:, oc, 0:512], ALU.add, [PSB[kb], B_xt], [B_xt])
            proj_fm(CH["o"], lambda kc: AT[:, kc, :], [B_at], cons_res3)
            rmsnorm_deferred(XT, "g3", XN, B_xn, B_xt, SQ3, B_sq3, RS3, B_rs3)
            mg = mlp_gen(CH["mlp1"], XT, B_xt, HID3, B_hid3, TMP3, B_tmp3, RS3, B_rs3)
            if i + 1 < NTL:
                attn_loads(i + 1)
                units = attn_units(i + 1)
            else:
                units = []
            done = False
            for u in units:
                u()
                for _ in range(4):
                    if next(mg, "end") == "end":
                        done = True
                        break
            if not done:
                for _ in mg:
                    pass
            ACT(SQ3[:, :, 0:512], XT[:, :, 0:512], AF.Square, [B_xt], [B_sq3])
            kb = newbank()
            for c in range(8):
                MM(PS[kb][:, 0:512], ONESB, SQ3[:, c, 0:512], c == 0, c == 7, [B_sq3, B_const], [PSB[kb]])
            ACT(RS3[:, 0:512], PS[kb][:, 0:512], AF.Sqrt, [PSB[kb], B_const], [B_rs3], scale=1.0 / D, bias=EPSC)
            P.op("dve", lambda e: e.reciprocal(out=RS3[:, 0:512], in_=RS3[:, 0:512]), [B_rs3], [B_rs3])
            def fin_stt(b):
                yb = b % 2
                for c in range(8):
                    STT(YTB[yb][:, c, :], XT[:, c, b * 128:(b + 1) * 128], vcol("g4", c), RS3[:, b * 128:(b + 1) * 128],
                        ALU.mult, ALU.mult, [B_xt, B_rs3, B_const], [B_ytb[yb]])

            def fin_out(b):
                yb = b % 2
                k0, k1 = newbank(), newbank()
                for c in range(8):
                    kbb = k0 if c < 4 else k1
                    TR(PS[kbb][:, (c % 4) * 128:(c % 4 + 1) * 128], YTB[yb][:, c, :], IDF, [B_ytb[yb], B_const], [PSB[kbb]])
                CP("act", OUTS[yb][:, 0:512], PS[k0][:, 0:512], [PSB[k0]], [B_outs[yb]])
                CP("act", OUTS[yb][:, 512:1024], PS[k1][:, 0:512], [PSB[k1]], [B_outs[yb]])
                DMA("pool", yc[t0 + b * 128:t0 + (b + 1) * 128, :], OUTS[yb], [B_outs[yb]], [Buf("yout")])
            fin_stt(0)
            fin_stt(1)
            fin_out(0)
            fin_stt(2)
            fin_out(1)
            fin_stt(3)
            fin_out(2)
            fin_out(3)

        P.finalize()
        P.emit()
    return nc, P


def _fm(v):
    return np.ascontiguousarray(np.asarray(v, np.float32).reshape(8, 128).T)


def make_vecs(inp, flags):
    vecs = np.zeros((128, NV), np.float32)

    def put(name, v):
        vecs[:, VC[name]:VC[name] + 8] = _fm(v)
    put("g0", inp["l0_norm_mix"])
    put("g1", inp["l0_norm_ffn"])
    put("g2", inp["l1_norm_mix"])
    put("g3", inp["l1_norm_ffn"])
    put("g4", inp["final_norm"])
    for k in range(4):
        put("cw%d" % k, inp["l0_conv_w"][k])
    put("cb", inp["l0_conv_b"])
    for dn, pre in (("f", "l0_fwd_"), ("b", "l0_bwd_")):
        put("ba_" + dn, np.asarray(inp[pre + "ba"]).reshape(-1))
        put("bx_" + dn, np.asarray(inp[pre + "bx"]).reshape(-1))
        put("lam_" + dn, inp[pre + "lam"])
    for s, f in enumerate(flags):
        vecs[:, VC["flags"] + s] = f
    return vecs


def make_btab(rpb):
    rpb = np.asarray(rpb, np.float32)
    c = np.arange(64)
    cs = np.clip(c - 8, 0, 48)
    kc = np.arange(64)
    valid = (kc[:, None] >= cs[None, :]) & (kc[:, None] < cs[None, :] + 16)
    rel = np.clip(kc[:, None] - c[None, :] + 15, 0, 30)
    tab = np.full((2, 64, 7, 16, 2, 64), NEG, np.float32)
    for o in range(-3, 4):
        for b in range(2):
            for a in range(2):
                dr = 2 * o + b - a
                if dr < -7 or dr > 7:
                    continue
                g = rpb[:, dr + 7, :][:, rel]
                g = np.where(valid[None], g, np.float32(NEG))
                tab[b, :, o + 3, :, a, :] = np.transpose(g, (1, 0, 2))
    return np.ascontiguousarray(tab.reshape(128, 7 * 16 * 128))


def make_rowmask(seq_rows):
    nrows = sum(seq_rows)
    NP = nrows // 2
    starts = np.cumsum([0] + list(seq_rows))
    seq_of = np.zeros(nrows, np.int64)
    for s in range(len(seq_rows)):
        seq_of[starts[s]:starts[s + 1]] = s
    rm = np.full((2, NP, 7, 2), NEG, np.float32)
    for j in range(NP):
        for a in range(2):
            R = 2 * j + a
            s = seq_of[R]
            r0, n = starts[s], seq_rows[s]
            rs = int(np.clip(R - r0 - 4, 0, n - 8)) + r0
            for o in range(-3, 4):
                for b in range(2):
                    kr = 2 * (j + o) + b
                    if rs <= kr < rs + 8:
                        rm[b, j, o + 3, a] = 0.0
    out = np.repeat(rm.reshape(2, 1, NP * 14), 64, axis=1).reshape(128, NP * 14)
    return np.ascontiguousarray(out)


def make_xhalo(xcore, seq_tokens):
    NT = xcore.shape[0]
    NTL = NT // T
    starts = np.cumsum([0] + list(seq_tokens))
    seq_of = np.zeros(NT, np.int64)
    for s in range(len(seq_tokens)):
        seq_of[starts[s]:starts[s + 1]] = s
    xh = np.zeros((NTL, 3, D), np.float32)
    for i in range(NTL):
        t0 = i * T
        for m, tok in enumerate((t0 - 2, t0 - 1, t0 + T)):
            ref = t0 if m < 2 else t0 + T - 1
            if 0 <= tok < NT and seq_of[tok] == seq_of[ref]:
                xh[i, m] = xcore[tok]
    xh = xh.reshape(NTL, 3, 8, 128).transpose(0, 3, 2, 1).reshape(NTL, 128, 24)
    return np.ascontiguousarray(xh)


def core_inputs(inp, xcore, seq_tokens, shared):
    nseg = xcore.shape[0] // 2048
    starts = set(np.cumsum([0] + list(seq_tokens)).tolist())
    flags = [0.0 if (s * 2048) in starts else 1.0 for s in range(nseg)]
    m = dict(shared)
    m["xc"] = np.ascontiguousarray(xcore, dtype=np.float32)
    m["xhalo"] = make_xhalo(xcore, seq_tokens)
    m["vecs"] = make_vecs(inp, flags)
    m["rowmask"] = make_rowmask([t // 64 for t in seq_tokens])
    return m


def shared_inputs(inp):
    f = lambda k: np.ascontiguousarray(np.asarray(inp[k], np.float32))
    return {
        "btab": make_btab(inp["l1_rpb"]),
        "w_in": f("l0_w_in"), "w_out0": f("l0_w_out"), "w_up0": f("l0_w_up"), "w_dn0": f("l0_w_down"),
        "w_qkv": f("l1_w_qkv"), "w_o": f("l1_w_o"), "w_up1": f("l1_w_up"), "w_dn1": f("l1_w_down"),
        "wa_f": f("l0_fwd_wa"), "wx_f": f("l0_fwd_wx"), "wa_b": f("l0_bwd_wa"), "wx_b": f("l0_bwd_wx"),
    }


_CACHE = {}


def get_program(nseg):
    if nseg not in _CACHE:
        _CACHE[nseg] = build_program(nseg)[0]
    return _CACHE[nseg]


def kernel(**inp):
    xp = np.asarray(inp["x_prompt"], np.float32)
    xs = np.asarray(inp["x_sample"], np.float32)
    shared = shared_inputs(inp)
    in_maps = []
    for k in range(4):
        in_maps.append(core_inputs(inp, xs[k], [8192], shared))
    for p in range(4):
        xcore = np.zeros((8192, D), np.float32)
        xcore[0:2048] = xp[2 * p]
        xcore[2048:4096] = xp[2 * p + 1]
        in_maps.append(core_inputs(inp, xcore, [2048, 2048, 2048, 2048], shared))
    nc = get_program(4)
    res = run_bass_kernel_spmd(nc, in_maps, core_ids=list(range(8)))
    y_s = np.stack([np.asarray(res.results[k]["yc"], np.float32) for k in range(4)], axis=0)
    y_p = np.zeros((8, 2048, D), np.float32)
    for p in range(4):
        yc = np.asarray(res.results[4 + p]["yc"], np.float32)
        y_p[2 * p] = yc[0:2048]
        y_p[2 * p + 1] = yc[2048:4096]
    return (y_p, y_s)
```

```python
import contextlib
import numpy as np
import concourse.bass as bass
import concourse.mybir as mybir
from concourse.bass_utils import run_bass_kernel_spmd

F32 = mybir.dt.float32
BF16 = mybir.dt.bfloat16
AF = mybir.ActivationFunctionType
ALU = mybir.AluOpType

D = 1024
T = 512
NEG = -30000.0
EPS = 1e-6
GC0 = 0.7978845608028654
GC1 = 0.044715

ENGS = ("pe", "act", "dve", "pool", "sp")
DMA_SLOTS = {"sp": 16, "pool": 10, "act": 2}


def _flat(lst):
    out = []
    for x in lst:
        if isinstance(x, (list, tuple)):
            out.extend(_flat(x))
        else:
            out.append(x)
    return out


class Buf:
    __slots__ = ("name", "lw", "rd")

    def __init__(self, name):
        self.name = name
        self.lw = None
        self.rd = []


class Op:
    __slots__ = ("eng", "emit", "deps", "is_dma", "milestone", "sem", "val", "idx")

    def __init__(self, eng, emit, is_dma):
        self.eng = eng
        self.emit = emit
        self.deps = set()
        self.is_dma = is_dma
        self.milestone = False
        self.sem = None
        self.val = None


class Prog:
    def __init__(self, nc):
        self.nc = nc
        self.ops = []
        self.last = {e: None for e in ENGS}
        self.dmas_since_bar = []
        self.bar = {e: set() for e in ENGS}

    def op(self, eng, emit, reads=(), writes=(), dma=False):
        o = Op(eng, emit, dma)
        reads = _flat(reads)
        writes = _flat(writes)
        idx = len(self.ops)
        o.idx = idx
        deps = o.deps
        for b in reads:
            if b.lw is not None:
                deps.add(b.lw)
        for b in writes:
            if b.lw is not None:
                deps.add(b.lw)
            deps.update(b.rd)
        if self.bar[eng]:
            deps.update(self.bar[eng])
            self.bar[eng] = set()
        if eng == "pe" and not dma:
            ops = self.ops
            o.deps = deps = {d for d in deps if ops[d].is_dma or ops[d].eng != "pe"}
        deps.discard(idx)
        ops_ = self.ops
        best = {}
        red = set()
        for d in deps:
            od = ops_[d]
            if od.is_dma:
                red.add(d)
            elif od.eng not in best or best[od.eng] < d:
                best[od.eng] = d
        red.update(best.values())
        o.deps = deps = red
        for b in reads:
            b.rd.append(idx)
        for b in writes:
            b.lw = idx
            b.rd = []
        self.ops.append(o)
        if dma:
            self.dmas_since_bar.append(idx)
        else:
            self.last[eng] = idx
        return o

    def barrier(self):
        s = set(self.dmas_since_bar)
        for e in ENGS:
            if self.last[e] is not None:
                s.add(self.last[e])
        self.dmas_since_bar = []
        for e in ENGS:
            self.bar[e] = set(s)

    def finalize(self, final_wait_eng="sp"):
        nc = self.nc
        ops = self.ops
        for o in ops:
            if o.is_dma:
                o.milestone = True
            for d in o.deps:
                ops[d].milestone = True
        self._stack = contextlib.ExitStack()
        st = self._stack
        eng_sem = {e: st.enter_context(nc.semaphore("s_" + e)) for e in ENGS}
        dma_sems = {e: [st.enter_context(nc.semaphore("d_%s_%d" % (e, i))) for i in range(n)]
                    for e, n in DMA_SLOTS.items()}
        eng_cnt = {e: 0 for e in ENGS}
        dma_cnt = {e: 0 for e in DMA_SLOTS}
        slot_uses = {e: [0] * n for e, n in DMA_SLOTS.items()}
        slot_prev = {e: [None] * n for e, n in DMA_SLOTS.items()}
        streams = {e: [] for e in ENGS}
        known = {e: {} for e in ENGS}

        def add_waits(e, plist):
            best = {}
            for p in plist:
                k = id(p.sem)
                if k not in best or best[k][1] < p.val:
                    best[k] = (p.sem, p.val)
            for k, (sem, val) in best.items():
                if known[e].get(k, 0) >= val:
                    continue
                known[e][k] = val
                streams[e].append(("wait", sem, val))

        last_dma = {}
        for o in ops:
            e = o.eng
            plist = [ops[d] for d in o.deps]
            if o.is_dma:
                n = DMA_SLOTS[e]
                i = dma_cnt[e] % n
                dma_cnt[e] += 1
                prev = slot_prev[e][i]
                if prev is not None:
                    plist.append(prev)
                add_waits(e, plist)
                slot_uses[e][i] += 1
                o.sem = dma_sems[e][i]
                o.val = 16 * slot_uses[e][i]
                slot_prev[e][i] = o
                last_dma[(e, i)] = o
                streams[e].append(("dma", o))
            else:
                add_waits(e, plist)
                if o.milestone:
                    eng_cnt[e] += 1
                    o.sem = eng_sem[e]
                    o.val = eng_cnt[e]
                streams[e].append(("op", o))
        add_waits(final_wait_eng, list(last_dma.values()))
        self.streams = streams
        self.stats = {e: len(streams[e]) for e in ENGS}
        self.max_sem = dict(eng_cnt)

    def emit(self):
        nc = self.nc
        streams = self.streams
        engmap = {"pe": "tensor", "act": "scalar", "dve": "vector", "pool": "gpsimd", "sp": "sync"}
        with nc.Block() as block:
            for e in ENGS:
                def body(eng, e=e):
                    for item in streams[e]:
                        if item[0] == "wait":
                            eng.wait_ge(item[1], item[2])
                        elif item[0] == "dma":
                            o = item[1]
                            o.emit(eng).then_inc(o.sem, 16)
                        else:
                            o = item[1]
                            ins = o.emit(eng)
                            if o.milestone:
                                ins.then_inc(o.sem, 1)
                getattr(block, engmap[e])(body)
        self._stack.close()


VC = {}
_off = 0
for _n, _w in [("g0", 8), ("g1", 8), ("g2", 8), ("g3", 8), ("g4", 8),
               ("cw0", 8), ("cw1", 8), ("cw2", 8), ("cw3", 8), ("cb", 8),
               ("ba_f", 8), ("bx_f", 8), ("lam_f", 8), ("ba_b", 8), ("bx_b", 8), ("lam_b", 8),
               ("flags", 4)]:
    VC[_n] = _off
    _off += _w
NV = _off

CH = {}
_c = 0
for _n, _k in [("in_xr", 8), ("in_gate", 8), ("out0", 8), ("mlp0", 64), ("q", 8), ("k", 8), ("v", 8),
               ("o", 8), ("mlp1", 64)]:
    CH[_n] = _c
    _c += _k
NCH = _c


def build_program(nseg, stop_after=None):
    NT = 2048 * nseg
    NTL = NT // T
    NP = NT // 128
    nc = bass.Bass("TRN2", target_bir_lowering=False)

    def din(name, shape, dt=F32):
        return nc.dram_tensor(name, list(shape), dt, kind="ExternalInput").ap()

    def dscr(name, shape, dt):
        return nc.dram_tensor(name, list(shape), dt, kind="Internal").ap()

    xc = din("xc", [NT, D])
    xhalo = din("xhalo", [NTL, 128, 24])
    vecs = din("vecs", [128, NV])
    rowmask = din("rowmask", [128, NP * 14])
    btab = din("btab", [128, 7 * 16 * 128])
    g4row = din("g4row", [128, D])
    w_in = din("w_in", [D, 2 * D])
    w_out0 = din("w_out0", [D, D])
    w_up0 = din("w_up0", [D, 4 * D])
    w_dn0 = din("w_dn0", [4 * D, D])
    w_qkv = din("w_qkv", [D, 3 * D])
    w_o = din("w_o", [D, D])
    w_up1 = din("w_up1", [D, 4 * D])
    w_dn1 = din("w_dn1", [4 * D, D])
    gw = {k: din(k, [16, 64, 64]) for k in ("wa_f", "wx_f", "wa_b", "wx_b")}
    yc = nc.dram_tensor("yc", [NT, D], F32, kind="ExternalOutput").ap()

    WS = dscr("WS", [NCH, 128, 1024], BF16)
    HB_S = dscr("HB_S", [NTL, 128, 4096], BF16)
    X2_S = dscr("X2_S", [128, 8, NT], F32)
    QT_S = dscr("QT_S", [128, 8, NT], BF16)
    KT_S = dscr("KT_S", [128, 8, NT], BF16)
    V_S = dscr("V_S", [NP, 128, 1040], BF16)
    XT_S = dscr("XT_S", [128, 8, NT], F32)
    XN_S = dscr("XN_S", [128, 8, NT], BF16)
    XR_S = dscr("XR_S", [128, 8, NT], F32)

    st = contextlib.ExitStack()
    with st:
        NW = 52800
        SB = st.enter_context(nc.sbuf_tensor("arena", [128, NW], F32))
        PS = [st.enter_context(nc.psum_tensor("ps%d" % k, [128, 512], F32)) for k in range(8)]
        PSB = [Buf("ps%d" % k) for k in range(8)]
        P = Prog(nc)

        cur = [0]

        def alloc(words):
            o = cur[0]
            cur[0] += (words + 7) // 8 * 8
            assert cur[0] <= NW, ("SBUF arena overflow", cur[0])
            return o

        def f32v(off, n):
            return SB[:, off:off + n]

        def bf16v(off, nwords):
            return SB[:, off:off + nwords].bitcast(BF16)

        o_vecs = alloc(NV)
        VEC = f32v(o_vecs, NV)
        o_der = alloc(64)
        DER = f32v(o_der, 64)
        o_misc = alloc(16)
        MISC = f32v(o_misc, 16)
        o_carry = alloc(16)
        CARRY = f32v(o_carry, 16)
        IDF = f32v(alloc(128), 128)
        IDB = bf16v(alloc(64), 64)
        ONESB = bf16v(alloc(64), 64)
        o_rm = alloc(NP * 14)
        RM = f32v(o_rm, NP * 14)
        NSLOT = 8
        o_wr = alloc(NSLOT * 512)
        WR = [bf16v(o_wr + s * 512, 512).rearrange("p (k j) -> p k j", k=8) for s in range(NSLOT)]
        WRB = [Buf("wr%d" % s) for s in range(NSLOT)]
        o_xt = alloc(8 * 516)
        XT = f32v(o_xt, 8 * 516).rearrange("p (c t) -> p c t", c=8)
        o_xn = alloc(4 * 516)
        XN = bf16v(o_xn, 4 * 516).rearrange("p (c t) -> p c t", c=8)
        persist_end = cur[0]
        GWB = {k: bf16v(alloc(512), 512).rearrange("p (c j) -> p c j", c=8) for k in gw}
        gwb_end = cur[0]

        B_const = Buf("const")
        B_xt = Buf("xt")
        B_xn = Buf("xn")
        CUR = {"XT": XT, "XN": XN, "B_xt": B_xt, "B_xn": B_xn, "cast_eng": "pool", "b_eng": "pool"}
        B_carry = Buf("carry")
        WSB = [Buf("ws%d" % c) for c in range(NCH)]
        HBSB = [Buf("hbs%d" % i) for i in range(NTL)]
        X2SB = [Buf("x2s%d" % i) for i in range(NTL)]
        QSB = [Buf("qs%d" % i) for i in range(NTL)]
        KSB = [Buf("ks%d" % i) for i in range(NTL)]
        VSB = [Buf("vs%d" % i) for i in range(NTL)]
        XTSB = [Buf("xts%d" % i) for i in range(NTL)]
        XNSB = [Buf("xns%d" % i) for i in range(NTL)]
        XRSB = [Buf("xrs%d" % i) for i in range(NTL)]

        def MM(out, lhsT, rhs, start, stop, r, w):
            P.op("pe", lambda e: e.matmul(out, lhsT=lhsT, rhs=rhs, start=start, stop=stop), r, w)

        def TR(out, in_, ident, r, w):
            P.op("pe", lambda e: e.transpose(out=out, in_=in_, identity=ident), r, w)

        def ACT(out, in_, func, r, w, scale=1.0, bias=0.0):
            P.op("act", lambda e: e.activation(out=out, in_=in_, func=func, scale=scale, bias=bias), r, w)

        def TT(eng, out, in0, in1, op, r, w):
            P.op(eng, lambda e: e.tensor_tensor(out=out, in0=in0, in1=in1, op=op), r, w)

        def TS(eng, out, in0, s1, s2, op0, op1, r, w):
            if op1 is None:
                P.op(eng, lambda e: e.tensor_scalar(out=out, in0=in0, scalar1=s1, scalar2=None, op0=op0), r, w)
            else:
                P.op(eng, lambda e: e.tensor_scalar(out=out, in0=in0, scalar1=s1, scalar2=s2, op0=op0, op1=op1), r, w)

        def STT(out, in0, scalar, in1, op0, op1, r, w):
            P.op("dve", lambda e: e.scalar_tensor_tensor(out=out, in0=in0, scalar=scalar, in1=in1, op0=op0, op1=op1), r, w)

        def CP(eng, out, in_, r, w):
            if eng == "act":
                P.op("act", lambda e: e.activation(out=out, in_=in_, func=AF.Copy), r, w)
            else:
                P.op(eng, lambda e: e.tensor_copy(out=out, in_=in_), r, w)

        def MS(eng, ap, val, w):
            P.op(eng, lambda e: e.memset(ap, val), (), w)

        def DMA(q, out, in_, r, w):
            P.op(q, lambda e: e.dma_start(out=out, in_=in_), r, w, dma=True)

        bank_rr = [0]
        bank_pool = [list(range(6))]

        def newbank():
            lst = bank_pool[0]
            k = lst[bank_rr[0] % len(lst)]
            bank_rr[0] += 1
            return k

        wr_rr = [0]

        def wload(ch):
            s = wr_rr[0] % NSLOT
            wr_rr[0] += 1
            DMA("sp", WR[s].rearrange("p k j -> p (k j)"), WS[ch], [WSB[ch]], [WRB[s]])
            return WR[s], WRB[s]

        evac_rr = [0]

        def evac_eng():
            evac_rr[0] += 1
            return "act" if evac_rr[0] % 2 else "dve"

        DMA("sp", VEC, vecs, [], [B_const])
        DMA("sp", RM, rowmask, [], [B_const])
        MS("pool", MISC[:, 0:1], EPS, [B_const])
        MS("pool", MISC[:, 1:2], 1.0, [B_const])
        MS("pool", IDF, 0.0, [B_const])
        P.op("pool", lambda e: e.affine_select(out=IDF, in_=IDF, pattern=[[-1, 128]], compare_op=ALU.not_equal,
                                               fill=1.0, base=0, channel_multiplier=1), [B_const], [B_const])
        CP("pool", IDB, IDF, [B_const], [B_const])
        MS("pool", ONESB, 1.0, [B_const])
        EPSC = MISC[:, 0:1]
        ONEC = MISC[:, 1:2]

        def vcol(name, c=None):
            o = VC[name]
            if c is None:
                return VEC[:, o:o + 8]
            return VEC[:, o + c:o + c + 1]

        import os
        _skip = os.environ.get("KSKIP", "")
        DERI = {}
        for di, dn in enumerate(("f", "b")):
            base = di * 32
            hba = DER[:, base:base + 8]
            hbx = DER[:, base + 8:base + 16]
            cc = DER[:, base + 16:base + 24]
            hc = DER[:, base + 24:base + 32]
            if "der" in _skip:
                DERI[dn] = (base, base + 8, base + 16, base + 24)
                continue
            TS("dve", hba, vcol("ba_" + dn), 0.5, None, ALU.mult, None, [B_const], [B_const])
            TS("dve", hbx, vcol("bx_" + dn), 0.5, None, ALU.mult, None, [B_const], [B_const])
            ACT(cc, vcol("lam_" + dn), AF.Exp, [B_const], [B_const], scale=-1.0)
            ACT(cc, cc, AF.Ln, [B_const], [B_const], scale=1.0, bias=ONEC)
            TS("dve", hc, cc, -4.0, None, ALU.mult, None, [B_const], [B_const])
            TS("dve", cc, cc, -8.0, None, ALU.mult, None, [B_const], [B_const])
            DERI[dn] = (base, base + 8, base + 16, base + 24)

        def dcol(dn, which, c):
            o = DERI[dn][which] + c
            return DER[:, o:o + 1]

        cur[0] = gwb_end
        GST = f32v(alloc(1024), 1024).rearrange("p (c j) -> p c j", c=8)
        B_gst = Buf("gst")
        for k in ("wa_b", "wx_b", "wa_f", "wx_f"):
            if "gst" in _skip:
                break
            MS("pool", GST, 0.0, [B_gst])
            src = gw[k].rearrange("(c t) i j -> t i c j", t=2)
            DMA("sp", GST[0:64, :, 0:64], src[0], [], [B_gst])
            DMA("sp", GST[64:128, :, 64:128], src[1], [], [B_gst])
            CP("dve", GWB[k], GST, [B_gst], [B_const])

        conv_list = []

        def add_conv(name, W, row0, col0, n, stride_cols=128):
            for i in range(n):
                conv_list.append((CH[name] + i, W, row0, col0 + i * stride_cols))

        add_conv("in_xr", w_in, 0, 0, 8)
        add_conv("in_gate", w_in, 0, 1024, 8)
        add_conv("out0", w_out0, 0, 0, 8)
        for q in range(4):
            for f in range(8):
                conv_list.append((CH["mlp0"] + q * 16 + f, w_up0, 0, q * 1024 + f * 128))
            for oc in range(8):
                conv_list.append((CH["mlp0"] + q * 16 + 8 + oc, w_dn0, q * 1024, oc * 128))
        add_conv("q", w_qkv, 0, 0, 8)
        add_conv("k", w_qkv, 0, 1024, 8)
        add_conv("v", w_qkv, 0, 2048, 8)
        add_conv("o", w_o, 0, 0, 8)
        for q in range(4):
            for f in range(8):
                conv_list.append((CH["mlp1"] + q * 16 + f, w_up1, 0, q * 1024 + f * 128))
            for oc in range(8):
                conv_list.append((CH["mlp1"] + q * 16 + 8 + oc, w_dn1, q * 1024, oc * 128))
        conv_pos = [0]

        def emit_conv(n):
            for _ in range(n):
                if conv_pos[0] >= len(conv_list):
                    return
                ch, W, r0, c0 = conv_list[conv_pos[0]]
                conv_pos[0] += 1
                src = W[r0:r0 + 1024, c0:c0 + 128].rearrange("(kc p) j -> p kc j", p=128)
                dst = WS[ch].rearrange("p (kc j) -> p kc j", kc=8)
                DMA("pool", dst, src, [], [WSB[ch]])

        if "conv" not in _skip:
            emit_conv(8)
        P.barrier()
        if stop_after == "prologue":
            P.finalize(); P.emit()
            return nc, P

        PS7B = [Buf("ps7_%d" % c) for c in range(8)]
        PS7N = Buf("ps7n")

        def l0_scratch(tag):
            d = {}
            d["XIN"] = alloc(4096)
            for k in ("R1", "R2", "R3", "R4", "R5"):
                d[k] = alloc(8 * 516)
            d["XRB"] = alloc(2048)
            d["RS"] = alloc(520)
            d["XHT"] = alloc(24)
            B = {"XIN": Buf(tag + "xin"), "RS": Buf(tag + "rs"), "XHT": Buf(tag + "xht")}
            for k in ("R1", "R2", "R3", "R4", "R5", "XRB"):
                B[k] = [Buf("%s%s_%d" % (tag, k, c)) for c in range(8)]
            d["B"] = B
            for k in ("R1", "R2", "R3", "R4", "R5"):
                d["v" + k] = f32v(d[k], 8 * 516).rearrange("p (c t) -> p c t", c=8)
            d["vXRB"] = bf16v(d["XRB"], 2048).rearrange("p (c t) -> p c t", c=8)
            return d

        def rmsnorm(src, gname, ncols, out, out_buf, src_buf, SQ, B_sq, RS, B_rs):
            ACT(SQ[:, :, 0:ncols], src[:, :, 0:ncols], AF.Square, [src_buf], [B_sq])
            kb = newbank()
            for c in range(8):
                MM(PS[kb][:, 0:512], ONESB, SQ[:, c, 0:512], c == 0, c == 7, [B_sq, B_const], [PSB[kb]])
            ACT(RS[:, 0:512], PS[kb][:, 0:512], AF.Sqrt, [PSB[kb], B_const], [B_rs], scale=1.0 / D, bias=EPSC)
            if ncols > 512:
                for c in range(8):
                    MM(PS[7][:, 64:64 + ncols - 512], ONESB, SQ[:, c, 512:ncols], c == 0, c == 7, [B_sq, B_const], [PSB[7]])
                ACT(RS[:, 512:ncols], PS[7][:, 64:64 + ncols - 512], AF.Sqrt, [PSB[7], B_const], [B_rs],
                    scale=1.0 / D, bias=EPSC)
            P.op("dve", lambda e: e.reciprocal(out=RS[:, 0:ncols], in_=RS[:, 0:ncols]), [B_rs], [B_rs])
            for c in range(8):
                STT(out[:, c, 0:ncols], src[:, c, 0:ncols], vcol(gname, c), RS[:, 0:ncols], ALU.mult, ALU.mult,
                    [src_buf, B_rs, B_const], [out_buf])

        def rmsnorm_deferred(src, gname, out, out_buf, src_buf, SQ, B_sq, RS, B_rs):
            for c in range(8):
                if c < 4:
                    ACT(out[:, c, 0:512], src[:, c, 0:512], AF.Identity, [src_buf, B_const], [out_buf], scale=vcol(gname, c))
                else:
                    TS("dve", out[:, c, 0:512], src[:, c, 0:512], vcol(gname, c), None, ALU.mult, None,
                       [src_buf, B_const], [out_buf])
            ACT(SQ[:, :, 0:512], src[:, :, 0:512], AF.Square, [src_buf], [B_sq])
            kb = newbank()
            for c in range(8):
                MM(PS[kb][:, 0:512], ONESB, SQ[:, c, 0:512], c == 0, c == 7, [B_sq, B_const], [PSB[kb]])
            ACT(RS[:, 0:512], PS[kb][:, 0:512], AF.Sqrt, [PSB[kb], B_const], [B_rs], scale=1.0 / D, bias=EPSC)
            P.op("dve", lambda e: e.reciprocal(out=RS[:, 0:512], in_=RS[:, 0:512]), [B_rs], [B_rs])

        def l0_front_parts(i, L, SQ, B_sq):
            B = L["B"]
            XIN = [f32v(L["XIN"] + b * 1024, 1024) for b in range(4)]
            XHT = f32v(L["XHT"], 24).rearrange("p (c m) -> p c m", c=8)
            RS = f32v(L["RS"], 520)
            t0 = i * T
            ncols = 515

            def fa(part):
                if part == 0:
                    for b in range(4):
                        DMA("sp", XIN[b], xc[t0 + b * 128:t0 + (b + 1) * 128, :], [], [B["XIN"]])
                    DMA("sp", XHT.rearrange("p c m -> p (c m)"), xhalo[i], [], [B["XHT"]])
                for c in range(2 * part, 2 * part + 2):
                    kb = newbank()
                    for b in range(4):
                        TR(PS[kb][:, b * 128:(b + 1) * 128], XIN[b][:, c * 128:(c + 1) * 128], IDF, [B["XIN"], B_const], [PSB[kb]])
                    CP("act", XT[:, c, 0:512], PS[kb][:, 0:512], [PSB[kb]], [B_xt])
                if part == 3:
                    CP("act", XT[:, :, 512:515], XHT, [B["XHT"]], [B_xt])

            def fb():
                ACT(SQ[:, :, 0:ncols], XT[:, :, 0:ncols], AF.Square, [B_xt], [B_sq])
                kb = newbank()
                for c in range(8):
                    MM(PS[kb][:, 0:512], ONESB, SQ[:, c, 0:512], c == 0, c == 7, [B_sq, B_const], [PSB[kb]])
                ACT(RS[:, 0:512], PS[kb][:, 0:512], AF.Sqrt, [PSB[kb], B_const], [B["RS"]], scale=1.0 / D, bias=EPSC)
                for c in range(8):
                    MM(PS[7][:, 64:64 + ncols - 512], ONESB, SQ[:, c, 512:ncols], c == 0, c == 7, [B_sq, B_const], [PSB[7]])
                ACT(RS[:, 512:ncols], PS[7][:, 64:64 + ncols - 512], AF.Sqrt, [PSB[7], B_const], [B["RS"]],
                    scale=1.0 / D, bias=EPSC)
                P.op("dve", lambda e: e.reciprocal(out=RS[:, 0:ncols], in_=RS[:, 0:ncols]), [B["RS"]], [B["RS"]])

            def fc():
                for c in range(8):
                    STT(XN[:, c, 0:ncols], XT[:, c, 0:ncols], vcol("g0", c), RS[:, 0:ncols], ALU.mult, ALU.mult,
                        [B_xt, B["RS"], B_const], [B_xn])
            return [lambda: fa(0), lambda: fa(1), lambda: fa(2), lambda: fa(3), fb, fc]

        def l0_gate_branch(L, G, BG):
            B = L["B"]
            XG, T3, T4 = L["vR5"], L["vR3"], L["vR4"]

            def s0(c):
                w, wb = wload(CH["in_gate"] + c)
                kb = newbank()
                for kc in range(8):
                    MM(PS[kb][:, 0:512], w[:, kc, :], CUR["XN"][:, kc, 0:512], kc == 0, kc == 7, [wb, CUR["B_xn"]], [PSB[kb]])
                CP("act", XG[:, c, 0:512], PS[kb][:, 0:512], [PSB[kb]], [B["R5"][c]])
                ACT(T4[:, c, 0:512], XG[:, c, 0:512], AF.Square, [B["R5"][c]], [B["R4"][c]])
                TS("dve", T4[:, c, 0:512], T4[:, c, 0:512], GC1, 1.0, ALU.mult, ALU.add, [B["R4"][c]], [B["R4"][c]])
                TT("pool", T4[:, c, 0:512], T4[:, c, 0:512], XG[:, c, 0:512], ALU.mult, [B["R4"][c], B["R5"][c]], [B["R4"][c]])

            def s1(c):
                ACT(T3[:, c, 0:512], T4[:, c, 0:512], AF.Tanh, [B["R4"][c]], [B["R3"][c]], scale=GC0)
                STT(G[:, c, :], T3[:, c, 0:512], 1.0, XG[:, c, 0:512], ALU.add, ALU.mult, [B["R3"][c], B["R5"][c]], [BG[c]])

            for k in range(9):
                if k < 8:
                    s0(k)
                if k >= 1:
                    s1(k - 1)

        def l0_xr_a(dn, L, have_xr=False, gate=None, hooks=None):
            B = L["B"]
            U, XR, RT, IT, M, XRB = L["vR1"], L["vR2"], L["vR3"], L["vR4"], L["vR5"], L["vXRB"]
            A = U
            ga, gx = GWB["wa_" + dn], GWB["wx_" + dn]

            def s0(c):
                if have_xr:
                    CP("pool", XRB[:, c, :], XR[:, c, 0:512], [B["R2"][c]], [B["XRB"][c]])
                    return
                w, wb = wload(CH["in_xr"] + c)
                kb = newbank()
                for kc in range(8):
                    MM(PS[kb][:, 0:512], w[:, kc, :], CUR["XN"][:, kc, 0:512], kc == 0, kc == 7, [wb, CUR["B_xn"]], [PSB[kb]])
                hb = 6 + (c % 2)
                for kc in range(8):
                    MM(PS[hb][:, c * 4:c * 4 + 3], w[:, kc, :], CUR["XN"][:, kc, 512:515], kc == 0, kc == 7, [wb, CUR["B_xn"]], [PSB[hb]])
                CP("act", U[:, c, 2:514], PS[kb][:, 0:512], [PSB[kb]], [B["R1"][c]])
                CP("dve", U[:, c, 0:2], PS[hb][:, c * 4:c * 4 + 2], [PSB[hb]], [B["R1"][c]])
                CP("dve", U[:, c, 514:515], PS[hb][:, c * 4 + 2:c * 4 + 3], [PSB[hb]], [B["R1"][c]])
                TS("dve", XR[:, c, 0:512], U[:, c, 0:512], vcol("cw0", c), vcol("cb", c), ALU.mult, ALU.add,
                   [B["R1"][c], B_const], [B["R2"][c]])
                for k in (1, 2, 3):
                    STT(XR[:, c, 0:512], U[:, c, k:k + 512], vcol("cw%d" % k, c), XR[:, c, 0:512], ALU.mult, ALU.add,
                        [B["R1"][c], B["R2"][c], B_const], [B["R2"][c]])

            def s0b(c):
                CP(CUR["cast_eng"], XRB[:, c, :], XR[:, c, 0:512], [B["R2"][c]], [B["XRB"][c]])

            def s1(c):
                k1 = newbank()
                MM(PS[k1][:, 0:512], ga[:, c, :], XRB[:, c, :], True, True, [B["XRB"][c], B_const], [PSB[k1]])
                k2 = newbank()
                MM(PS[k2][:, 0:512], gx[:, c, :], XRB[:, c, :], True, True, [B["XRB"][c], B_const], [PSB[k2]])
                ACT(RT[:, c, 0:512], PS[k1][:, 0:512], AF.Tanh, [PSB[k1], B_const], [B["R3"][c]], scale=0.5, bias=dcol(dn, 0, c))
                ACT(IT[:, c, 0:512], PS[k2][:, 0:512], AF.Tanh, [PSB[k2], B_const], [B["R4"][c]], scale=0.5, bias=dcol(dn, 1, c))
                ACT(A[:, c, 0:512], RT[:, c, 0:512], AF.Exp, [B["R3"][c], B_const], [B["R1"][c]],
                    scale=dcol(dn, 3, c), bias=dcol(dn, 3, c))
                ACT(M[:, c, 0:512], RT[:, c, 0:512], AF.Exp, [B["R3"][c], B_const], [B["R5"][c]],
                    scale=dcol(dn, 2, c), bias=dcol(dn, 2, c))
                STT(IT[:, c, 0:512], IT[:, c, 0:512], 1.0, XR[:, c, 0:512], ALU.add, ALU.mult, [B["R4"][c], B["R2"][c]], [B["R4"][c]])

            LAG = 2
            if gate is not None:
                G_, BG_ = gate
                XGv = U
                gs = {}

                def g0(c):
                    w, wb = wload(CH["in_gate"] + c)
                    kb = newbank()
                    for kc in range(8):
                        MM(PS[kb][:, 0:512], w[:, kc, :], CUR["XN"][:, kc, 0:512], kc == 0, kc == 7, [wb, CUR["B_xn"]], [PSB[kb]])
                    gs[c] = kb

                def g1(c):
                    kb = gs[c]
                    CP("dve", XGv[:, c, 0:512], PS[kb][:, 0:512], [PSB[kb]], [B["R1"][c]])
                    TT("dve", RT[:, c, 0:512], XGv[:, c, 0:512], XGv[:, c, 0:512], ALU.mult, [B["R1"][c]], [B["R3"][c]])
                    TS("dve", RT[:, c, 0:512], RT[:, c, 0:512], GC1, 1.0, ALU.mult, ALU.add, [B["R3"][c]], [B["R3"][c]])
                    TT("pool", RT[:, c, 0:512], RT[:, c, 0:512], XGv[:, c, 0:512], ALU.mult, [B["R3"][c], B["R1"][c]], [B["R3"][c]])

                def g2(c):
                    ACT(RT[:, c, 0:512], RT[:, c, 0:512], AF.Tanh, [B["R3"][c]], [B["R3"][c]], scale=GC0)
                    STT(G_[:, c, :], RT[:, c, 0:512], 1.0, XGv[:, c, 0:512], ALU.add, ALU.mult, [B["R3"][c], B["R1"][c]], [BG_[c]])
                for k in range(8 + 3):
                    if k < 8:
                        g0(k)
                        s0(k)
                    if 1 <= k < 9:
                        g1(k - 1)
                    if 2 <= k < 10:
                        g2(k - 2)
                    if k >= 3:
                        s1(k - 3)
            else:
                for k in range(8 + 3):
                    if k < 8:
                        s0(k)
                    if 1 <= k < 9:
                        s0b(k - 1)
                    if k >= 3:
                        s1(k - 3)
                    if hooks and k in hooks:
                        hooks[k]()
            for hf in range(2):
                ACT(M[:, 4 * hf:4 * hf + 4, 0:512], M[:, 4 * hf:4 * hf + 4, 0:512], AF.Sqrt,
                    [B["R5"][4 * hf:4 * hf + 4], B_const], [B["R5"][4 * hf:4 * hf + 4]], scale=-1.0, bias=ONEC)

        def l0_xr_b(L, first, flag_col, reverse, post, post_b=None):
            B = L["B"]
            U, RT, IT, M = L["vR1"], L["vR3"], L["vR4"], L["vR5"]
            A, H = U, RT
            if not first and flag_col is not None:
                TS("dve", CARRY[:, 0:8], CARRY[:, 0:8], flag_col, None, ALU.mult, None, [B_carry, B_const], [B_carry])
            for c in range(8):
                TT(CUR["b_eng"], IT[:, c, 0:512], IT[:, c, 0:512], M[:, c, 0:512], ALU.mult, [B["R4"][c], B["R5"][c]], [B["R4"][c]])
            for c in range(8):
                init = 0.0 if first else CARRY[:, c:c + 1]
                if reverse:
                    o_, a_, b_ = H[:, c, 511::-1], A[:, c, 511::-1], IT[:, c, 511::-1]
                else:
                    o_, a_, b_ = H[:, c, 0:512], A[:, c, 0:512], IT[:, c, 0:512]
                P.op("dve", (lambda o_, a_, b_, init: (lambda e: e.tensor_tensor_scan(
                    out=o_, data0=a_, data1=b_, initial=init, op0=ALU.mult, op1=ALU.add)))(o_, a_, b_, init),
                    [B["R1"][c], B["R4"][c], B_carry], [B["R3"][c]])
                col = 0 if reverse else 511
                CP("dve", CARRY[:, c:c + 1], H[:, c, col:col + 1], [B["R3"][c]], [B_carry])
                post(c)
                if post_b is not None and c >= 1:
                    post_b(c - 1)
            if post_b is not None:
                post_b(7)

        def proj_fm(chbase, rhs_fn, rhs_bufs, consume):
            for oc in range(8):
                w, wb = wload(chbase + oc)
                kb = newbank()
                for kc in range(8):
                    MM(PS[kb][:, 0:512], w[:, kc, :], rhs_fn(kc), kc == 0, kc == 7, [wb] + rhs_bufs, [PSB[kb]])
                consume(oc, kb, w, wb)

        def mlp_gen(chbase, xres, xres_buf, HID, HIDB, TMP, TMPB, RS=None, B_rs=None):
            def up(q):
                for f in range(8):
                    w, wb = wload(chbase + q * 16 + f)
                    kb = newbank()
                    for kc in range(8):
                        MM(PS[kb][:, 0:512], w[:, kc, :], CUR["XN"][:, kc, 0:512], kc == 0, kc == 7, [wb, CUR["B_xn"]], [PSB[kb]])
                    t = f % 2
                    if RS is None:
                        ACT(TMP[t], PS[kb][:, 0:512], AF.Relu, [PSB[kb]], [TMPB[t]])
                    else:
                        STT(TMP[t], PS[kb][:, 0:512], 0.0, RS[:, 0:512], ALU.max, ALU.mult, [PSB[kb], B_rs], [TMPB[t]])
                        ACT(HID[q % 2][:, f, :], TMP[t], AF.Square, [TMPB[t]], [HIDB[q % 2]])
                        yield
                        continue
                    TT("pool", HID[q % 2][:, f, :], TMP[t], TMP[t], ALU.mult, [TMPB[t]], [HIDB[q % 2]])
                    yield

            def down(q):
                for oc in range(8):
                    w, wb = wload(chbase + q * 16 + 8 + oc)
                    kb = newbank()
                    for f in range(8):
                        MM(PS[kb][:, 0:512], w[:, f, :], HID[q % 2][:, f, :], f == 0, f == 7, [wb, HIDB[q % 2]], [PSB[kb]])
                    TT("dve", xres[:, oc, 0:512], PS[kb][:, 0:512], xres[:, oc, 0:512], ALU.add,
                       [PSB[kb], xres_buf], [xres_buf])
                    yield

            yield from up(0)
            for q in range(4):
                if q + 1 < 4:
                    yield from up(q + 1)
                yield from down(q)

        def mlp(*a):
            for _ in mlp_gen(*a):
                pass

        def seg_flag(tile_start_tok):
            seg = tile_start_tok // 2048
            return VEC[:, VC["flags"] + seg:VC["flags"] + seg + 1]

        cur[0] = gwb_end
        L = l0_scratch("p1")
        HBO = bf16v(alloc(2048), 2048).rearrange("p (c t) -> p c t", c=8)
        B_hbo = Buf("hbo")
        bank_pool[0] = list(range(6))
        per_tile_conv = (len(conv_list) - 8 + NTL - 1) // NTL
        if "convfirst" in _skip:
            emit_conv(len(conv_list))
        SQ1 = bf16v(alloc(4 * 516), 4 * 516).rearrange("p (c t) -> p c t", c=8)
        B_sq1 = Buf("sq1")
        order = list(range(NTL - 1, -1, -1))

        CUR.update({"cast_eng": "act", "b_eng": "dve"})
        c3 = (per_tile_conv + 2) // 3

        def front1_parts(i):
            parts = l0_front_parts(i, L, SQ1, B_sq1)

            def stores():
                DMA("pool", XT_S[:, :, i * T:(i + 1) * T], XT[:, :, 0:512], [B_xt], [XTSB[i]])
                DMA("pool", XN_S[:, :, i * T:(i + 1) * T], XN[:, :, 0:512], [B_xn], [XNSB[i]])
                emit_conv(c3)
            return parts, stores
        parts0, stores0 = front1_parts(order[0])
        for f in parts0:
            f()
        stores0()
        _maxn = int(os.environ.get("KMAXN", "999"))
        for n, i in enumerate(order):
            if n >= _maxn:
                break
            first = (n == 0)
            flag_col = None
            if not first and ((i + 1) * T) % 2048 == 0:
                flag_col = seg_flag((i + 1) * T)
            if n + 1 < NTL:
                nparts, nstores = front1_parts(order[n + 1])
                hooks = {1: nparts[0], 2: nparts[1], 3: nparts[2], 4: nparts[3], 6: nparts[4]}
            else:
                nparts, nstores, hooks = None, None, None
            l0_xr_a("b", L, hooks=hooks)
            DMA("pool", XR_S[:, :, i * T:(i + 1) * T], L["vR2"][:, :, 0:512], [L["B"]["R2"]], [XRSB[i]])
            emit_conv(c3)
            if nparts is not None:
                nparts[5]()
                nstores()

            def post1(c, L=L):
                CP("act", HBO[:, c, :], L["vR3"][:, c, 0:512], [L["B"]["R3"][c]], [B_hbo])
            l0_xr_b(L, first, flag_col, True, post1)
            DMA("pool", HB_S[i], HBO.rearrange("p c t -> p (c t)"), [B_hbo], [HBSB[i]])
            emit_conv(c3)
        emit_conv(len(conv_list))
        CUR.update({"cast_eng": "pool", "b_eng": "dve"})
        P.barrier()
        if stop_after == "pass1":
            P.finalize(); P.emit()
            return nc, P

        cur[0] = gwb_end
        bank_pool[0] = list(range(8))
        L = l0_scratch("p2")
        B = L["B"]
        G = bf16v(alloc(2048), 2048).rearrange("p (c t) -> p c t", c=8)
        HBL = bf16v(alloc(2048), 2048).rearrange("p (c t) -> p c t", c=8)
        TMP = [f32v(alloc(512), 512) for _ in range(2)]
        XT_R = [XT, f32v(L["XIN"], 4096).rearrange("p (c t) -> p c t", c=8)]
        XN_R = [XN, bf16v(alloc(2048), 2048).rearrange("p (c t) -> p c t", c=8)]
        BXT_R = [B_xt, Buf("xt1")]
        BXN_R = [B_xn, Buf("xn1")]
        BG = [Buf("g%d" % c) for c in range(8)]
        B_hbl = Buf("hbl")
        TMPB = [Buf("tmp0"), Buf("tmp1")]
        HID = [bf16v(L["R2"] + h * 2064, 2048).rearrange("p (c t) -> p c t", c=8) for h in range(2)]
        HIDB = [B["R2"][0:4], B["R2"][4:8]]
        QT = bf16v(L["R5"], 2048).rearrange("p (c t) -> p c t", c=8)
        KT = bf16v(L["R5"] + 2064, 2048).rearrange("p (c t) -> p c t", c=8)
        VST = bf16v(L["R4"], 4 * 520).rearrange("p (j c) -> p j c", j=4)
        BVST = B["R4"][0:5]
        Y = L["vXRB"]
        SQ2 = bf16v(L["R1"], 4 * 516).rearrange("p (c t) -> p c t", c=8)
        RS2 = f32v(L["RS"], 520)
        RC4 = f32v(alloc(8), 8)
        B_rc4 = Buf("rc4")

        def loads2(i):
            r = i % 2
            DMA("sp", XT_R[r][:, :, 0:512], XT_S[:, :, i * T:(i + 1) * T], [XTSB[i]], [BXT_R[r]])
            DMA("sp", XN_R[r][:, :, 0:512], XN_S[:, :, i * T:(i + 1) * T], [XNSB[i]], [BXN_R[r]])
            DMA("sp", HBL.rearrange("p c t -> p (c t)"), HB_S[i], [HBSB[i]], [B_hbl])

        loads2(0)
        for i in range(NTL):
            first = (i == 0)
            flag_col = None
            if not first and (i * T) % 2048 == 0:
                flag_col = seg_flag(i * T)
            r = i % 2
            XTc, XNc, BXT, BXN = XT_R[r], XN_R[r], BXT_R[r], BXN_R[r]
            CUR.update({"XT": XTc, "XN": XNc, "B_xt": BXT, "B_xn": BXN})
            DMA("sp", L["vR2"][:, :, 0:512], XR_S[:, :, i * T:(i + 1) * T], [XRSB[i]], [B["R2"]])

            def post2a(c):
                Hc = L["vR3"][:, c, 0:512]
                TT("pool", Hc, Hc, HBL[:, c, :], ALU.add, [B["R3"][c], B_hbl], [B["R3"][c]])

            def post2b(c):
                Hc = L["vR3"][:, c, 0:512]
                STT(Y[:, c, :], Hc, 0.25, G[:, c, :], ALU.mult, ALU.mult, [B["R3"][c], BG[c]], [B["XRB"][c]])
            l0_xr_a("f", L, have_xr=True, gate=(G, BG))
            ows = [wload(CH["out0"] + oc) for oc in range(8)]
            okb = [newbank() for _ in range(8)]

            def post2c(c):
                post2b(c)
                for oc in range(8):
                    MM(PS[okb[oc]][:, 0:512], ows[oc][0][:, c, :], Y[:, c, :], c == 0, c == 7,
                       [ows[oc][1], B["XRB"][c]], [PSB[okb[oc]]])
            l0_xr_b(L, first, flag_col, False, post2a, post2c)
            for oc in range(8):
                TT("dve", XTc[:, oc, 0:512], PS[okb[oc]][:, 0:512], XTc[:, oc, 0:512], ALU.add, [PSB[okb[oc]], BXT], [BXT])
            if i + 1 < NTL:
                loads2(i + 1)
            rmsnorm_deferred(XTc, "g1", XNc, BXN, BXT, SQ2, B["R1"][0:4], RS2, B["RS"])
            mlp(CH["mlp0"], XTc, BXT, HID, HIDB, TMP, TMPB, RS2, B["RS"])
            DMA("pool", X2_S[:, :, i * T:(i + 1) * T], XTc[:, :, 0:512], [BXT], [X2SB[i]])
            rmsnorm_deferred(XTc, "g2", XNc, BXN, BXT, SQ2, B["R1"][0:4], RS2, B["RS"])
            kbv = newbank()
            for pr in range(4):
                for c in range(8):
                    MM(PS[kbv][:, pr:pr + 1], SQ2[:, c, pr * 128:(pr + 1) * 128], ONESB[:, 0:1], c == 0, c == 7,
                       [B["R1"][0:4], B_const], [PSB[kbv]])
            ACT(RC4[:, 0:4], PS[kbv][:, 0:4], AF.Sqrt, [PSB[kbv], B_const], [B_rc4], scale=1.0 / D, bias=EPSC)
            P.op("dve", lambda e: e.reciprocal(out=RC4[:, 0:4], in_=RC4[:, 0:4]), [B_rc4], [B_rc4])

            def cons_q(oc, kb, w, wb):
                TT("dve", QT[:, oc, :], PS[kb][:, 0:512], RS2[:, 0:512], ALU.mult, [PSB[kb], B["RS"]], [B["R5"][0:4]])

            def cons_k(oc, kb, w, wb):
                TT("dve", KT[:, oc, :], PS[kb][:, 0:512], RS2[:, 0:512], ALU.mult, [PSB[kb], B["RS"]], [B["R5"][4:8]])
            proj_fm(CH["q"], lambda kc: XNc[:, kc, 0:512], [BXN], cons_q)
            proj_fm(CH["k"], lambda kc: XNc[:, kc, 0:512], [BXN], cons_k)
            MS("pool", VST, 1.0, [BVST])
            for vg in range(2):
                kbs = [newbank() for _ in range(4)]
                for vcl in range(4):
                    w, wb = wload(CH["v"] + vg * 4 + vcl)
                    for pr in range(4):
                        for kc in range(8):
                            MM(PS[kbs[pr]][:, vcl * 128:(vcl + 1) * 128], XNc[:, kc, pr * 128:(pr + 1) * 128], w[:, kc, :],
                               kc == 0, kc == 7, [wb, BXN], [PSB[kbs[pr]]])
                for pr in range(4):
                    dst = VST[:, pr, :].rearrange("p (h c) -> p h c", c=65)[:, vg * 8:(vg + 1) * 8, 0:64]
                    srcv = PS[kbs[pr]][:, 0:512].rearrange("p (h c) -> p h c", c=64)
                    if pr % 2 == 0:
                        ACT(dst, srcv, AF.Identity, [PSB[kbs[pr]], B_rc4], [BVST], scale=RC4[:, pr:pr + 1])
                    else:
                        TS("dve", dst, srcv, RC4[:, pr:pr + 1], None, ALU.mult, None, [PSB[kbs[pr]], B_rc4], [BVST])
            DMA("pool", QT_S[:, :, i * T:(i + 1) * T], QT, [B["R5"][0:4]], [QSB[i]])
            DMA("pool", KT_S[:, :, i * T:(i + 1) * T], KT, [B["R5"][4:8]], [KSB[i]])
            DMA("pool", V_S[4 * i:4 * i + 4].rearrange("j p c -> p j c"), VST, [BVST], [VSB[i]])
        CUR.update({"XT": XT, "XN": XN, "B_xt": B_xt, "B_xn": B_xn})
        P.barrier()
        if stop_after == "pass2":
            P.finalize(); P.emit()
            return nc, P

        cur[0] = persist_end
        bank_pool[0] = list(range(8))
        QT3 = [bf16v(alloc(2048), 2048).rearrange("p (c t) -> p c t", c=8) for _ in range(2)]
        KTW = bf16v(alloc(5120), 5120).rearrange("p (c t) -> p c t", c=8)
        VW = bf16v(alloc(5200), 5200).rearrange("p (s c) -> p s c", s=10)
        BT = bf16v(alloc(7168), 7168).rearrange("p (o h q) -> p o h q", o=7, h=16)
        PB = [[bf16v(alloc(256), 256) for _ in range(6)] for _ in range(2)]
        AO = [bf16v(alloc(512), 512) for _ in range(4)]
        AT = bf16v(alloc(2048), 2048).rearrange("p (c t) -> p c t", c=8)
        HID3 = [bf16v(alloc(2048), 2048).rearrange("p (c t) -> p c t", c=8) for _ in range(2)]
        TMP3 = [f32v(alloc(512), 512) for _ in range(2)]
        RC = f32v(alloc(8), 8)
        RS3 = f32v(alloc(520), 520)
        SQ3 = bf16v(alloc(2048), 2048).rearrange("p (c t) -> p c t", c=8)
        G4R = f32v(alloc(1024), 1024)
        RC4F = f32v(alloc(8), 8)
        OUTS = [f32v(alloc(1024), 1024) for _ in range(2)]
        B_q3, B_kw, B_vw, B_bt = Buf("q3"), Buf("kw"), Buf("vw"), Buf("bt")
        B_pb = [Buf("pb0"), Buf("pb1")]
        B_ao = [Buf("ao%d" % k) for k in range(4)]
        B_at, B_rc, B_rs3, B_sq3 = Buf("at"), Buf("rc"), Buf("rs3"), Buf("sq3")
        B_g4r, B_rc4f = Buf("g4r"), Buf("rc4f")
        B_outs = [Buf("outs0"), Buf("outs1")]
        B_hid3 = [Buf("hid3a"), Buf("hid3b")]
        B_tmp3 = [Buf("tmp3a"), Buf("tmp3b")]
        DMA("pool", BT.rearrange("p o h q -> p (o h q)"), btab, [], [B_bt])
        DMA("sp", G4R, g4row, [], [B_g4r])
        MS("pool", QT3[0][64:128], 0.0, [B_q3])
        MS("pool", QT3[1][0:64], 0.0, [B_q3])

        def attn_loads(i):
            t0 = i * T
            DMA("sp", QT3[0][0:64], QT_S[0:64, :, t0:t0 + T], [QSB[i]], [B_q3])
            DMA("sp", QT3[1][64:128], QT_S[64:128, :, t0:t0 + T], [QSB[i]], [B_q3])
            p_lo = 4 * i - 3
            s0 = max(0, -p_lo)
            s1 = min(10, NP - p_lo)
            if s0 > 0:
                MS("pool", KTW[:, :, 0:s0 * 128], 0.0, [B_kw])
                MS("pool", VW[:, 0:s0, :], 0.0, [B_vw])
            if s1 < 10:
                MS("pool", KTW[:, :, s1 * 128:1280], 0.0, [B_kw])
                MS("pool", VW[:, s1:10, :], 0.0, [B_vw])
            tiles_touched = sorted(set((p_lo + s_) // 4 for s_ in range(s0, s1)))
            DMA("sp", KTW[:, :, s0 * 128:s1 * 128], KT_S[:, :, (p_lo + s0) * 128:(p_lo + s1) * 128],
                [KSB[t] for t in tiles_touched], [B_kw])
            DMA("sp", VW[:, s0:s1, :], V_S[p_lo + s0:p_lo + s1].rearrange("j p c -> p j c"),
                [VSB[t] for t in tiles_touched], [B_vw])

        def attn_units(i):
            p_lo = 4 * i - 3
            state = {"pend": None}

            def emit_pv(jl, hg, olist, par):
                ko = newbank()
                Ov = PS[ko][:, 0:260].rearrange("p (h c) -> p h c", c=65)
                for hh in range(4):
                    h = hg * 4 + hh
                    for oi, o in enumerate(olist):
                        slot = 4 * i + jl + o - p_lo
                        MM(Ov[:, hh, :], PB[par][oi][:, hh * 128:(hh + 1) * 128], VW[:, slot, h * 65:(h + 1) * 65],
                           oi == 0, oi == len(olist) - 1, [B_pb[par], B_vw], [PSB[ko]])
                P.op("dve", lambda e: e.reciprocal(out=RC[:, 0:4], in_=Ov[:, :, 64]), [PSB[ko]], [B_rc])
                dst = AO[jl].rearrange("p (h c) -> p h c", c=64)[:, hg * 4:(hg + 1) * 4, :]
                TT("dve", dst, Ov[:, :, 0:64], RC[:, 0:4].unsqueeze(2).to_broadcast([128, 4, 64]), ALU.mult,
                   [PSB[ko], B_rc], [B_ao[jl]])

            def emit_tr(jl):
                kb = newbank()
                pv = PS[kb][:].bitcast(BF16)
                for kc in range(8):
                    TR(pv[:, kc * 128:(kc + 1) * 128], AO[jl][:, kc * 128:(kc + 1) * 128], IDB, [B_ao[jl], B_const], [PSB[kb]])
                CP("act", AT[:, :, jl * 128:(jl + 1) * 128], pv.rearrange("p (c t) -> p c t", c=8), [PSB[kb]], [B_at])

            def flush():
                pend = state["pend"]
                if pend is not None:
                    emit_pv(*pend)
                    if pend[1] == 3:
                        emit_tr(pend[0])
                state["pend"] = None

            def step(sidx):
                jl, hg = sidx // 4, sidx % 4
                j = 4 * i + jl
                olist = [-2, -1, 0, 1, 2]
                if j % 16 == 15:
                    olist = [-3] + olist
                if j % 16 == 0:
                    olist = olist + [3]
                interior = (j % 16) not in (0, 1, 14, 15)
                par = sidx % 2
                for oi, o in enumerate(olist):
                    slot = j + o - p_lo
                    ks = newbank()
                    for hh in range(4):
                        h = hg * 4 + hh
                        MM(PS[ks][:, hh * 128:(hh + 1) * 128], KTW[:, h // 2, slot * 128:(slot + 1) * 128],
                           QT3[h % 2][:, h // 2, jl * 128:(jl + 1) * 128], True, True, [B_kw, B_q3], [PSB[ks]])
                    Sv = PS[ks][:, 0:512].rearrange("p (h q) -> p h q", h=4)
                    STT(Sv, Sv, 0.125, BT[:, o + 3, hg * 4:(hg + 1) * 4, :], ALU.mult, ALU.add, [PSB[ks], B_bt], [PSB[ks]])
                    if interior and o in (-1, 0, 1):
                        ACT(PB[par][oi], PS[ks][:, 0:512], AF.Exp, [PSB[ks]], [B_pb[par]], scale=1.0)
                    else:
                        for a in range(2):
                            colx = (j * 7 + (o + 3)) * 2 + a
                            src = PS[ks][:, 0:512].rearrange("p (h a c) -> p h a c", h=4, a=2)[:, :, a, :]
                            dst = PB[par][oi].rearrange("p (h a c) -> p h a c", h=4, a=2)[:, :, a, :]
                            ACT(dst, src, AF.Exp, [PSB[ks], B_const], [B_pb[par]], scale=1.0, bias=RM[:, colx:colx + 1])
                flush()
                state["pend"] = (jl, hg, olist, par)

            units = [(lambda sidx=sidx: step(sidx)) for sidx in range(16)]
            units.append(flush)
            return units

        attn_loads(0)
        for u in attn_units(0):
            u()
        for i in range(NTL):
            t0 = i * T
            DMA("sp", XT[:, :, 0:512], X2_S[:, :, t0:t0 + T], [X2SB[i]], [B_xt])

            def cons_res3(oc, kb, w, wb):
                TT("dve", XT[:, oc, 0:512], PS[kb][:, 0:512], XT[:, oc, 0:512], ALU.add, [PSB[kb], B_xt], [B_xt])
            proj_fm(CH["o"], lambda kc: AT[:, kc, :], [B_at], cons_res3)
            rmsnorm_deferred(XT, "g3", XN, B_xn, B_xt, SQ3, B_sq3, RS3, B_rs3)
            mg = mlp_gen(CH["mlp1"], XT, B_xt, HID3, B_hid3, TMP3, B_tmp3, RS3, B_rs3)
            if i + 1 < NTL:
                attn_loads(i + 1)
                units = attn_units(i + 1)
            else:
                units = []
            done = False
            for u in units:
                u()
                for _ in range(4):
                    if next(mg, "end") == "end":
                        done = True
                        break
            if not done:
                for _ in mg:
                    pass
            ACT(SQ3[:, :, 0:512], XT[:, :, 0:512], AF.Square, [B_xt], [B_sq3])
            kbr = newbank()
            for b in range(4):
                for c in range(8):
                    MM(PS[kbr][:, b:b + 1], SQ3[:, c, b * 128:(b + 1) * 128], ONESB[:, 0:1], c == 0, c == 7,
                       [B_sq3, B_const], [PSB[kbr]])
            ACT(RC4F[:, 0:4], PS[kbr][:, 0:4], AF.Sqrt, [PSB[kbr], B_const], [B_rc4f], scale=1.0 / D, bias=EPSC)
            P.op("dve", lambda e: e.reciprocal(out=RC4F[:, 0:4], in_=RC4F[:, 0:4]), [B_rc4f], [B_rc4f])
            for b in range(4):
                yb = b % 2
                k0, k1 = newbank(), newbank()
                for c in range(8):
                    kbb = k0 if c < 4 else k1
                    TR(PS[kbb][:, (c % 4) * 128:(c % 4 + 1) * 128], XT[:, c, b * 128:(b + 1) * 128], IDF, [B_xt, B_const], [PSB[kbb]])
                STT(OUTS[yb][:, 0:512], PS[k0][:, 0:512], RC4F[:, b:b + 1], G4R[:, 0:512], ALU.mult, ALU.mult,
                    [PSB[k0], B_rc4f, B_g4r], [B_outs[yb]])
                STT(OUTS[yb][:, 512:1024], PS[k1][:, 0:512], RC4F[:, b:b + 1], G4R[:, 512:1024], ALU.mult, ALU.mult,
                    [PSB[k1], B_rc4f, B_g4r], [B_outs[yb]])
                DMA("pool", yc[t0 + b * 128:t0 + (b + 1) * 128, :], OUTS[yb], [B_outs[yb]], [Buf("yout")])

        P.finalize()
        P.emit()
    return nc, P


def _fm(v):
    return np.ascontiguousarray(np.asarray(v, np.float32).reshape(8, 128).T)


def make_vecs(inp, flags):
    vecs = np.zeros((128, NV), np.float32)

    def put(name, v):
        vecs[:, VC[name]:VC[name] + 8] = _fm(v)
    put("g0", inp["l0_norm_mix"])
    put("g1", inp["l0_norm_ffn"])
    put("g2", inp["l1_norm_mix"])
    put("g3", inp["l1_norm_ffn"])
    put("g4", inp["final_norm"])
    for k in range(4):
        put("cw%d" % k, inp["l0_conv_w"][k])
    put("cb", inp["l0_conv_b"])
    for dn, pre in (("f", "l0_fwd_"), ("b", "l0_bwd_")):
        put("ba_" + dn, np.asarray(inp[pre + "ba"]).reshape(-1))
        put("bx_" + dn, np.asarray(inp[pre + "bx"]).reshape(-1))
        put("lam_" + dn, inp[pre + "lam"])
    for s, f in enumerate(flags):
        vecs[:, VC["flags"] + s] = f
    return vecs


def make_btab(rpb):
    rpb = np.asarray(rpb, np.float32)
    c = np.arange(64)
    cs = np.clip(c - 8, 0, 48)
    kc = np.arange(64)
    valid = (kc[:, None] >= cs[None, :]) & (kc[:, None] < cs[None, :] + 16)
    rel = np.clip(kc[:, None] - c[None, :] + 15, 0, 30)
    tab = np.full((2, 64, 7, 16, 2, 64), NEG, np.float32)
    for o in range(-3, 4):
        for b in range(2):
            for a in range(2):
                dr = 2 * o + b - a
                if dr < -7 or dr > 7:
                    continue
                g = rpb[:, dr + 7, :][:, rel]
                g = np.where(valid[None], g, np.float32(NEG))
                tab[b, :, o + 3, :, a, :] = np.transpose(g, (1, 0, 2))
    return np.ascontiguousarray(tab.reshape(128, 7 * 16 * 128))


def make_rowmask(seq_rows):
    nrows = sum(seq_rows)
    NP = nrows // 2
    starts = np.cumsum([0] + list(seq_rows))
    seq_of = np.zeros(nrows, np.int64)
    for s in range(len(seq_rows)):
        seq_of[starts[s]:starts[s + 1]] = s
    rm = np.full((2, NP, 7, 2), NEG, np.float32)
    for j in range(NP):
        for a in range(2):
            R = 2 * j + a
            s = seq_of[R]
            r0, n = starts[s], seq_rows[s]
            rs = int(np.clip(R - r0 - 4, 0, n - 8)) + r0
            for o in range(-3, 4):
                for b in range(2):
                    kr = 2 * (j + o) + b
                    if rs <= kr < rs + 8:
                        rm[b, j, o + 3, a] = 0.0
    out = np.repeat(rm.reshape(2, 1, NP * 14), 64, axis=1).reshape(128, NP * 14)
    return np.ascontiguousarray(out)


def make_xhalo(xcore, seq_tokens):
    NT = xcore.shape[0]
    NTL = NT // T
    starts = np.cumsum([0] + list(seq_tokens))
    seq_of = np.zeros(NT, np.int64)
    for s in range(len(seq_tokens)):
        seq_of[starts[s]:starts[s + 1]] = s
    xh = np.zeros((NTL, 3, D), np.float32)
    for i in range(NTL):
        t0 = i * T
        for m, tok in enumerate((t0 - 2, t0 - 1, t0 + T)):
            ref = t0 if m < 2 else t0 + T - 1
            if 0 <= tok < NT and seq_of[tok] == seq_of[ref]:
                xh[i, m] = xcore[tok]
    xh = xh.reshape(NTL, 3, 8, 128).transpose(0, 3, 2, 1).reshape(NTL, 128, 24)
    return np.ascontiguousarray(xh)


def core_inputs(inp, xcore, seq_tokens, shared):
    nseg = xcore.shape[0] // 2048
    starts = set(np.cumsum([0] + list(seq_tokens)).tolist())
    flags = [0.0 if (s * 2048) in starts else 1.0 for s in range(nseg)]
    m = dict(shared)
    m["xc"] = np.ascontiguousarray(xcore, dtype=np.float32)
    m["xhalo"] = make_xhalo(xcore, seq_tokens)
    m["vecs"] = make_vecs(inp, flags)
    m["rowmask"] = make_rowmask([t // 64 for t in seq_tokens])
    return m


def shared_inputs(inp):
    f = lambda k: np.ascontiguousarray(np.asarray(inp[k], np.float32))
    return {
        "btab": make_btab(inp["l1_rpb"]),
        "g4row": np.ascontiguousarray(np.broadcast_to(np.asarray(inp["final_norm"], np.float32)[None, :], (128, D))),
        "w_in": f("l0_w_in"), "w_out0": f("l0_w_out"), "w_up0": f("l0_w_up"), "w_dn0": f("l0_w_down"),
        "w_qkv": f("l1_w_qkv"), "w_o": f("l1_w_o"), "w_up1": f("l1_w_up"), "w_dn1": f("l1_w_down"),
        "wa_f": f("l0_fwd_wa"), "wx_f": f("l0_fwd_wx"), "wa_b": f("l0_bwd_wa"), "wx_b": f("l0_bwd_wx"),
    }


_CACHE = {}


def get_program(nseg):
    if nseg not in _CACHE:
        _CACHE[nseg] = build_program(nseg)[0]
    return _CACHE[nseg]


def kernel(**inp):
    xp = np.asarray(inp["x_prompt"], np.float32)
    xs = np.asarray(inp["x_sample"], np.float32)
    shared = shared_inputs(inp)
    in_maps = []
    for k in range(4):
        in_maps.append(core_inputs(inp, xs[k], [8192], shared))
    for p in range(4):
        xcore = np.zeros((8192, D), np.float32)
        xcore[0:2048] = xp[2 * p]
        xcore[2048:4096] = xp[2 * p + 1]
        in_maps.append(core_inputs(inp, xcore, [2048, 2048, 2048, 2048], shared))
    nc = get_program(4)
    res = run_bass_kernel_spmd(nc, in_maps, core_ids=list(range(8)))
    y_s = np.stack([np.asarray(res.results[k]["yc"], np.float32) for k in range(4)], axis=0)
    y_p = np.zeros((8, 2048, D), np.float32)
    for p in range(4):
        yc = np.asarray(res.results[4 + p]["yc"], np.float32)
        y_p[2 * p] = yc[0:2048]
        y_p[2 * p + 1] = yc[2048:4096]
    return (y_p, y_s)
```
